# Optimizing a Trainium2 kernel written in Bass

```python
import jax, jax.numpy as jnp
from jax import lax
import numpy as np

D_MODEL = 1024
BATCH = 16
SEQ = 2048
DEPTH = 2

GRID_W = 64
CTX_LEN = 256
EPS = 1e-6
D_BRANCH = D_MODEL // 2
N_BRANCH = 3
HEAD_DIM = 64
ATT_HEADS = D_BRANCH // HEAD_DIM
ATT_KV_HEADS = ATT_HEADS // 4
WINDOW = 128
ATT_BLOCK = 128
ROPE_BASE = 10000.0
HG_DK = 128
HG_HEADS = D_BRANCH // HG_DK
HG_DV = D_BRANCH // HG_HEADS
HG_CHUNK = 64
MLP_CHUNK = 128
MLP_GROUP_DIM = 128
MLP_GROUPS = D_BRANCH // MLP_GROUP_DIM
D_FF = -(-8 * D_MODEL // (3 * 256)) * 256
IN_SPLITS = (ATT_HEADS * HEAD_DIM, ATT_KV_HEADS * HEAD_DIM, ATT_KV_HEADS * HEAD_DIM,
             D_BRANCH, D_BRANCH, D_BRANCH, D_BRANCH, D_BRANCH,
             D_BRANCH, D_BRANCH, N_BRANCH * D_MODEL)
D_IN = sum(IN_SPLITS)

kernel_name = "hybrid_gated_parallel_dit_block"


def rmsnorm(x, g):
    xf = x.astype(jnp.float32)
    y = xf * lax.rsqrt(jnp.mean(xf * xf, axis=-1, keepdims=True) + EPS)
    return (y * g.astype(jnp.float32)).astype(x.dtype)


def layernorm(x, g, b):
    xf = x.astype(jnp.float32)
    mu = jnp.mean(xf, axis=-1, keepdims=True)
    var = jnp.mean(jnp.square(xf - mu), axis=-1, keepdims=True)
    y = (xf - mu) * lax.rsqrt(var + EPS)
    return (y * g.astype(jnp.float32) + b.astype(jnp.float32)).astype(x.dtype)


def split_proj(p):
    out, off = [], 0
    for w in IN_SPLITS:
        out.append(p[..., off:off + w])
        off += w
    return out


def heads(t, n):
    return t.reshape(t.shape[0], t.shape[1], n, -1)


def axial_rope(x):
    L = x.shape[1]
    rows_n = L // GRID_W
    rows = jnp.repeat(jnp.arange(rows_n), GRID_W).astype(jnp.float32)
    cols = jnp.tile(jnp.arange(GRID_W), rows_n).astype(jnp.float32)
    half = HEAD_DIM // 2
    nf = half // 2
    inv = ROPE_BASE ** (-jnp.arange(nf, dtype=jnp.float32) / nf)

    def rot(xp, pos):
        ang = pos[:, None] * inv[None, :]
        cos = jnp.cos(ang)[None, :, None, :]
        sin = jnp.sin(ang)[None, :, None, :]
        x1, x2 = xp[..., :nf], xp[..., nf:]
        return jnp.concatenate([x1 * cos - x2 * sin, x2 * cos + x1 * sin], axis=-1)

    out = jnp.concatenate([rot(x[..., :half], rows), rot(x[..., half:], cols)], axis=-1)
    return out.astype(x.dtype)


def sink_softmax(s, sink_b):
    full = jnp.concatenate([s, jnp.broadcast_to(sink_b, s.shape[:-1] + (1,))], axis=-1)
    return jax.nn.softmax(full, axis=-1)[..., :-1]


def window_attention(q, k, v, kc, vc, sink):
    B_, S_ = q.shape[0], q.shape[1]
    nblk = S_ // ATT_BLOCK
    rep = ATT_HEADS // ATT_KV_HEADS
    scale = HEAD_DIM ** -0.5
    band = ATT_BLOCK + 2 * WINDOW
    qg = q.reshape(B_, S_, ATT_KV_HEADS, rep, HEAD_DIM)
    pad = ((0, 0), (WINDOW, WINDOW), (0, 0), (0, 0))
    kp = jnp.pad(k, pad)
    vp = jnp.pad(v, pad)
    sink_b = sink.astype(jnp.float32).reshape(ATT_KV_HEADS, rep)[None, :, :, None, None]
    n_loc = band

    def block(j):
        start = j * ATT_BLOCK
        qb = lax.dynamic_slice_in_dim(qg, start, ATT_BLOCK, axis=1)
        kb = lax.dynamic_slice_in_dim(kp, start, band, axis=1)
        vb = lax.dynamic_slice_in_dim(vp, start, band, axis=1)
        qpos = start + jnp.arange(ATT_BLOCK)
        kpos = start - WINDOW + jnp.arange(band)
        mask = ((jnp.abs(qpos[:, None] - kpos[None, :]) <= WINDOW)
                & (kpos >= 0)[None, :] & (kpos < S_)[None, :])
        s_loc = jnp.einsum('bqgrd,bkgd->bgrqk', qb, kb).astype(jnp.float32) * scale
        s_loc = jnp.where(mask, s_loc, -jnp.inf)
        s_ctx = jnp.einsum('bqgrd,bkgd->bgrqk', qb, kc).astype(jnp.float32) * scale
        p = sink_softmax(jnp.concatenate([s_loc, s_ctx], axis=-1), sink_b).astype(v.dtype)
        out = (jnp.einsum('bgrqk,bkgd->bqgrd', p[..., :n_loc], vb)
               + jnp.einsum('bgrqk,bkgd->bqgrd', p[..., n_loc:], vc))
        return out.reshape(B_, ATT_BLOCK, ATT_HEADS * HEAD_DIM)

    out = lax.map(block, jnp.arange(nblk))
    return out.transpose(1, 0, 2, 3).reshape(B_, S_, ATT_HEADS * HEAD_DIM)


def context_attention(qc, kc, vc, sink):
    B_, Lc = qc.shape[0], qc.shape[1]
    rep = ATT_HEADS // ATT_KV_HEADS
    qg = qc.reshape(B_, Lc, ATT_KV_HEADS, rep, HEAD_DIM)
    s = jnp.einsum('bqgrd,bkgd->bgrqk', qg, kc).astype(jnp.float32) * HEAD_DIM ** -0.5
    sink_b = sink.astype(jnp.float32).reshape(ATT_KV_HEADS, rep)[None, :, :, None, None]
    p = sink_softmax(s, sink_b).astype(vc.dtype)
    out = jnp.einsum('bgrqk,bkgd->bqgrd', p, vc)
    return out.reshape(B_, Lc, ATT_HEADS * HEAD_DIM)


def forget_terms(z, lb):
    zf = z.astype(jnp.float32)
    lbf = lb.astype(jnp.float32)
    logf = jnp.logaddexp(jnp.log(lbf), jnp.log1p(-lbf) + jax.nn.log_sigmoid(zf))
    k = (1.0 - lbf) * jax.nn.sigmoid(-zf)
    return logf, k


def hgrn_inputs(bq, bzf, bzb, bi, lb_f, lb_b):
    q = jax.nn.silu(bq.astype(jnp.float32)) * HG_DK ** -0.5
    logf_f, k_f = forget_terms(bzf, lb_f)
    logf_b, k_b = forget_terms(bzb, lb_b)
    v = bi.astype(jnp.float32)
    return (heads(q, HG_HEADS), heads(k_f, HG_HEADS), heads(k_b, HG_HEADS), heads(v, HG_HEADS),
            heads(logf_f, HG_HEADS), heads(logf_b, HG_HEADS))


def gla_chunk_scan(q, k, v, logf, s0):
    B_, L, H, _ = q.shape
    DV = v.shape[-1]
    n = L // HG_CHUNK

    def chunks(t):
        return t.reshape(B_, n, HG_CHUNK, H, t.shape[-1]).transpose(1, 0, 3, 2, 4)

    lower = jnp.tril(jnp.ones((HG_CHUNK, HG_CHUNK), dtype=bool))[:, :, None]

    def step(state, inp):
        qc, kc, vc, gc = inp
        G = jnp.cumsum(gc, axis=2)
        inter = jnp.einsum('bhtk,bhkv->bhtv', qc * jnp.exp(G), state)
        diff = G[:, :, :, None, :] - G[:, :, None, :, :]
        decay = jnp.exp(jnp.where(lower, diff, -jnp.inf))
        att = jnp.einsum('bhtk,bhtsk,bhsk->bhts', qc, decay, kc)
        intra = jnp.einsum('bhts,bhsv->bhtv', att, vc)
        G_last = G[:, :, -1:, :]
        new = (jnp.exp(G_last[:, :, 0, :])[..., None] * state
               + jnp.einsum('bhsk,bhsv->bhkv', kc * jnp.exp(G_last - G), vc))
        return new, inter + intra

    s_fin, o = lax.scan(step, s0, (chunks(q), chunks(k), chunks(v), chunks(logf)))
    return o.transpose(1, 0, 3, 2, 4).reshape(B_, L, H, DV), s_fin


def hgrn2_bidir(q, k_f, k_b, v, logf_f, logf_b, s_f, s_b):
    o_f, sf = gla_chunk_scan(q, k_f, v, logf_f, s_f)
    flip = lambda t: jnp.flip(t, axis=1)
    o_b, sb = gla_chunk_scan(flip(q), flip(k_b), flip(v), flip(logf_b), s_b)
    return o_f + flip(o_b), sf, sb


def hgrn_readout(o, g, hg_g, dtype):
    of = o * lax.rsqrt(jnp.mean(o * o, axis=-1, keepdims=True) + EPS)
    of = of * hg_g.astype(jnp.float32).reshape(HG_HEADS, HG_DV)
    of = of.reshape(o.shape[0], o.shape[1], D_BRANCH)
    return (of * jax.nn.silu(g.astype(jnp.float32))).astype(dtype)


def chunk_mlp(u, v, v_g, v_b, ws, bs):
    B_, L = v.shape[0], v.shape[1]
    n = L // MLP_CHUNK
    vn = layernorm(v, v_g, v_b).reshape(B_, n, MLP_CHUNK, MLP_GROUPS, MLP_GROUP_DIM)
    s = jnp.einsum('gts,bnsgc->bntgc', ws, vn) + bs.T[:, :, None]
    return u * s.reshape(B_, L, D_BRANCH)


def merge(a, b, c, gate, w_br, w_o):
    g = jax.nn.sigmoid(gate.reshape(gate.shape[:-1] + (N_BRANCH, D_MODEL)))
    m = g[..., 0, :] * (a @ w_br[0]) + g[..., 1, :] * (b @ w_br[1]) + g[..., 2, :] * (c @ w_br[2])
    return m @ w_o


def hybrid_mixer(h, hc, w_in, sink, lb_f, lb_b, hg_g, v_g, v_b, ws, bs, w_br, w_o, need_ctx):
    aq, ak, av, bq, bzf, bzb, bi, bg, cu, cv, gt = split_proj(h @ w_in)
    aqc, akc, avc, bqc, bzfc, bzbc, bic, bgc, cuc, cvc, gtc = split_proj(hc @ w_in)
    kc = heads(akc, ATT_KV_HEADS)
    vc = heads(avc, ATT_KV_HEADS)
    a = window_attention(axial_rope(heads(aq, ATT_HEADS)), axial_rope(heads(ak, ATT_KV_HEADS)),
                         heads(av, ATT_KV_HEADS), kc, vc, sink)
    zero = jnp.zeros((hc.shape[0], HG_HEADS, HG_DK, HG_DV), jnp.float32)
    oc, s_f, s_b = hgrn2_bidir(*hgrn_inputs(bqc, bzfc, bzbc, bic, lb_f, lb_b), zero, zero)
    o, _, _ = hgrn2_bidir(*hgrn_inputs(bq, bzf, bzb, bi, lb_f, lb_b), s_f, s_b)
    b = hgrn_readout(o, bg, hg_g, h.dtype)
    c = chunk_mlp(cu, cv, v_g, v_b, ws, bs)
    y = merge(a, b, c, gt, w_br, w_o)
    if not need_ctx:
        return y, None
    ac = context_attention(heads(aqc, ATT_HEADS), kc, vc, sink)
    bc = hgrn_readout(oc, bgc, hg_g, hc.dtype)
    cc = chunk_mlp(cuc, cvc, v_g, v_b, ws, bs)
    yc = merge(ac, bc, cc, gtc, w_br, w_o)
    return y, yc


def swiglu(h, w_gate, w_up, w_down):
    return (jax.nn.silu(h @ w_gate) * (h @ w_up)) @ w_down


def setup_inputs(seed: int = 0) -> dict:
    key = jax.random.key(seed)
    ks = jax.random.split(key, 24)
    nrm = lambda k, shape, s: jax.random.normal(k, shape, jnp.float32) * s
    D = D_MODEL
    return {
        'x': nrm(ks[0], (BATCH, SEQ, D), 1.0),
        'c': nrm(ks[1], (BATCH, D), 1.0),
        'ctx': nrm(ks[2], (BATCH, CTX_LEN, D), 1.0),
        'c_ctx': nrm(ks[3], (D,), 1.0),
        'w_ada': nrm(ks[4], (DEPTH, D, 6 * D), 0.5 * D ** -0.5),
        'b_ada': nrm(ks[5], (DEPTH, 6 * D), 0.02),
        'norm1_g': 1.0 + nrm(ks[6], (DEPTH, D), 0.02),
        'norm2_g': 1.0 + nrm(ks[7], (DEPTH, D), 0.02),
        'w_in': nrm(ks[8], (DEPTH, D, D_IN), D ** -0.5),
        'attn_sink': nrm(ks[9], (DEPTH, ATT_HEADS), 1.0),
        'hg_lb_logits': nrm(ks[10], (DEPTH, 2, D_BRANCH), 1.0),
        'hg_norm_g': 1.0 + nrm(ks[11], (DEPTH, D_BRANCH), 0.02),
        'mlp_v_norm_g': 1.0 + nrm(ks[12], (DEPTH, D_BRANCH), 0.02),
        'mlp_v_norm_b': nrm(ks[13], (DEPTH, D_BRANCH), 0.02),
        'mlp_ws': nrm(ks[14], (DEPTH, MLP_GROUPS, MLP_CHUNK, MLP_CHUNK), MLP_CHUNK ** -0.5),
        'mlp_bs': 1.0 + nrm(ks[15], (DEPTH, MLP_GROUPS, MLP_CHUNK), 0.02),
        'w_branch': nrm(ks[16], (DEPTH, N_BRANCH, D_BRANCH, D), D_BRANCH ** -0.5),
        'w_out': nrm(ks[17], (DEPTH, D, D), D ** -0.5),
        'w_ffn_gate': nrm(ks[18], (DEPTH, D, D_FF), D ** -0.5),
        'w_ffn_up': nrm(ks[19], (DEPTH, D, D_FF), D ** -0.5),
        'w_ffn_down': nrm(ks[20], (DEPTH, D_FF, D), D_FF ** -0.5),
        'final_norm_g': 1.0 + nrm(ks[21], (D,), 0.02),
    }


def reference(x, c, ctx, c_ctx, w_ada, b_ada, norm1_g, norm2_g, w_in, attn_sink, hg_lb_logits,
              hg_norm_g, mlp_v_norm_g, mlp_v_norm_b, mlp_ws, mlp_bs, w_branch, w_out,
              w_ffn_gate, w_ffn_up, w_ffn_down, final_norm_g):
    lb = jnp.cumsum(jax.nn.softmax(hg_lb_logits.astype(jnp.float32), axis=0), axis=0)
    lb = lb - lb[0]
    sc = jax.nn.silu(c)[:, None, :]
    scc = jax.nn.silu(c_ctx)
    for l in range(DEPTH):
        need_ctx = l < DEPTH - 1
        sh1, sc1, g1, sh2, sc2, g2 = jnp.split(sc @ w_ada[l] + b_ada[l], 6, axis=-1)
        csh1, csc1, cg1, csh2, csc2, cg2 = jnp.split(scc @ w_ada[l] + b_ada[l], 6, axis=-1)
        h = rmsnorm(x, norm1_g[l]) * (1.0 + sc1) + sh1
        hc = rmsnorm(ctx, norm1_g[l]) * (1.0 + csc1) + csh1
        y, yc = hybrid_mixer(h, hc, w_in[l], attn_sink[l], lb[l, 0], lb[l, 1], hg_norm_g[l],
                             mlp_v_norm_g[l], mlp_v_norm_b[l], mlp_ws[l], mlp_bs[l],
                             w_branch[l], w_out[l], need_ctx)
        x = x + g1 * y
        h = rmsnorm(x, norm2_g[l]) * (1.0 + sc2) + sh2
        x = x + g2 * swiglu(h, w_ffn_gate[l], w_ffn_up[l], w_ffn_down[l])
        if need_ctx:
            ctx = ctx + cg1 * yc
            hc = rmsnorm(ctx, norm2_g[l]) * (1.0 + csc2) + csh2
            ctx = ctx + cg2 * swiglu(hc, w_ffn_gate[l], w_ffn_up[l], w_ffn_down[l])
    return rmsnorm(x, final_norm_g)
```

```python
import numpy as np
import ml_dtypes
import concourse.bass as bass
import concourse.mybir as mybir
from concourse.bass_utils import run_bass_kernel_spmd

F32 = mybir.dt.float32
BF16 = mybir.dt.bfloat16
AF = mybir.ActivationFunctionType
ALU = mybir.AluOpType

D = 1024
KC = 8
SEQ = 2048
CTX = 256
T = SEQ + CTX
NT = T // 128
DEPTH = 2
D_IN = 7424
D_FF = 2816
EPS = 1e-6
ENGS = ('pe', 'act', 'dve', 'pool')
EIDX = {e: i for i, e in enumerate(ENGS)}


class Buf:
    __slots__ = ('name', 'writers', 'wgrp', 'readers', 'parent', 'subs')

    def __init__(self, name, parent=None):
        self.name = name
        self.writers = []
        self.wgrp = None
        self.readers = {}
        self.parent = parent
        self.subs = {}

    def conflict(self):
        if self.parent is None:
            return [self] + list(self.subs.values())
        return [self, self.parent]

    def upd(self):
        if self.parent is None:
            return [self] + list(self.subs.values())
        return [self]


class Chan:
    __slots__ = ('sem', 'count', 'wait_all', 'name')

    def __init__(self, name, wait_all=False):
        self.name = name
        self.sem = None
        self.count = 0
        self.wait_all = wait_all


class Op:
    __slots__ = ('eng', 'fn', 'deps', 'pos', 'is_dma', 'chan', 'chan_need', 'waits',
                 'needs_inc', 'semval', 'clk', 'done', 'name')


class V:
    __slots__ = ('ap', 'buf', 'tt')

    def __init__(self, ap, buf, tt):
        self.ap = ap
        self.buf = buf
        self.tt = tt

    def __getitem__(self, idx):
        return V(self.ap[idx], self.buf, self.tt)

    def rr(self, s, **kw):
        return V(self.ap.rearrange(s, **kw), self.buf, self.tt)

    def bc(self, shape):
        return V(self.ap.to_broadcast(shape), self.buf, self.tt)

    def bitcast(self, dt):
        return V(self.ap.bitcast(dt), self.buf, self.tt)

    def unsq(self, ax):
        return V(self.ap.unsqueeze(ax), self.buf, self.tt)

    def pbc(self, n):
        return V(self.ap.partition_broadcast(n), self.buf, self.tt)


class TT:
    def __init__(self, handle, name, dram=False):
        self.h = handle
        self.name = name
        self.root = Buf(name)
        self.chan = None
        self.dram = dram

    def full(self):
        return self.h.ap() if self.dram else self.h[:]

    def __getitem__(self, idx):
        return V(self.full(), self.root, self)[idx]

    def k(self, key):
        b = self.root.subs.get(key)
        if b is None:
            b = Buf(f"{self.name}.{key}", self.root)
            self.root.subs[key] = b
        return V(self.full(), b, self)

    def v(self):
        return V(self.full(), self.root, self)


class Prog:
    def __init__(self):
        self.nc = bass.Bass("TRN2", target_bir_lowering=False)
        self.ops = []
        self.eng_ops = {e: [] for e in ENGS + ('sp',)}
        self.npos = {e: 0 for e in ENGS}
        self.chans = []
        self.misc = self.new_chan('misc', wait_all=True)
        self.sb_off = 0
        self.outs = []
        self.tok_prev = None
        self.tok_cur = None

    def new_chan(self, name, wait_all=False):
        c = Chan(name, wait_all)
        self.chans.append(c)
        return c

    def dram(self, name, shape, dt, kind="Internal"):
        return TT(self.nc.dram_tensor(name, list(shape), dt, kind=kind), name, dram=True)

    def sbuf(self, name, shape, dt, off=None):
        if off is None:
            h = self.nc.alloc_sbuf_tensor(name, list(shape), dt)
        else:
            h = self.nc.alloc_sbuf_tensor_at(name, list(shape), dt, offset=off)
        return TT(h, name)

    def psum(self, name, shape, dt=F32):
        return TT(self.nc.alloc_psum_tensor(name, list(shape), dt), name)

    def alias(self, new, olds):
        for o in olds:
            for b in o.root.conflict():
                for w in b.writers:
                    new.root.readers[('w', id(w))] = w
                for k, r in b.readers.items():
                    new.root.readers[(k, id(r))] = r

    def step_begin(self):
        self.tok_prev = self.tok_cur
        self.tok_cur = [len(self.ops), len(self.eng_ops['pool'])]

    def _add(self, eng, fn, reads, writes, is_dma=False, chan=None, grp=None, name=None, hoist=False):
        op = Op()
        op.eng = eng
        op.fn = fn
        op.is_dma = is_dma
        op.chan = chan
        op.name = name
        op.needs_inc = False
        op.waits = None
        deps = {}
        gonly = None
        if isinstance(grp, tuple) and len(grp) == 3 and grp[0] == 'only':
            gonly = grp[1]
            grp = grp[2]

        def gfor(b):
            return grp if (gonly is None or b is gonly) else None
        for v in reads:
            for b in v.buf.conflict():
                for w in b.writers:
                    deps[id(w)] = w
        same_ok = False
        for v in writes:
            for b in v.buf.conflict():
                if not (gfor(b) is not None and b is v.buf and b.wgrp == gfor(b)):
                    for w in b.writers:
                        if same_ok and (not w.is_dma) and w.eng == eng:
                            continue
                        deps[id(w)] = w
                for r in b.readers.values():
                    if same_ok and (not r.is_dma) and r.eng == eng:
                        continue
                    deps[id(r)] = r
        op.deps = list(deps.values())
        op.chan_need = {}
        for d in op.deps:
            if d.is_dma:
                op.chan_need[id(d.chan)] = (d.chan, d.chan.count)
        for v in writes:
            for b in v.buf.upd():
                if gfor(b) is not None and b is v.buf and b.wgrp == gfor(b):
                    b.writers.append(op)
                else:
                    b.writers = [op]
                    b.wgrp = gfor(b) if b is v.buf else None
                b.readers = {}
        rkey = chan if is_dma else eng
        for v in reads:
            for b in v.buf.upd():
                b.readers[id(rkey) if is_dma else rkey] = op
        if is_dma:
            chan.count += 1
            op.pos = chan.count
        else:
            self.npos[eng] += 1
            op.pos = self.npos[eng]
        if hoist and self.tok_prev is not None:
            gi, pi = self.tok_prev
            self.ops.insert(gi, op)
            self.eng_ops[eng].insert(pi, op)
            self.tok_prev[0] += 1; self.tok_prev[1] += 1
            self.tok_cur[0] += 1; self.tok_cur[1] += 1
        else:
            self.ops.append(op)
            self.eng_ops[eng].append(op)
        return op

    def dma(self, out, in_, q='sp', chan=None, grp=None, hoist=False):
        if chan is None:
            tt = out.tt if not out.tt.dram else in_.tt
            if tt.chan is None:
                tt.chan = self.new_chan(tt.name)
            chan = tt.chan
        oa, ia = out.ap, in_.ap
        return self._add(q, lambda e: e.dma_start(out=oa, in_=ia), [in_], [out], True, chan, grp, hoist=hoist)

    def mm(self, out, lhsT, rhs, start=True, stop=True, grp=None, tp=None):
        oa, la, ra = out.ap, lhsT.ap, rhs.ap
        kw = {}
        if tp is not None:
            kw['tile_position'] = tp
        return self._add('pe', lambda e: e.matmul(oa, la, ra, start=start, stop=stop, **kw),
                         [lhsT, rhs], [out], grp=grp)

    def tr(self, out, in_, ident, grp=None):
        oa, ia, da = out.ap, in_.ap, ident.ap
        return self._add('pe', lambda e: e.transpose(oa, ia, da), [in_, ident], [out], grp=grp)

    def act(self, out, in_, func, bias=None, scale=None, accum=None, grp=None):
        oa, ia = out.ap, in_.ap
        kw = {}
        reads = [in_]
        writes = [out]
        if bias is not None:
            if isinstance(bias, V):
                kw['bias'] = bias.ap
                reads.append(bias)
            else:
                kw['bias'] = float(bias)
        if scale is not None:
            if isinstance(scale, V):
                kw['scale'] = scale.ap
                reads.append(scale)
            else:
                kw['scale'] = float(scale)
        if accum is not None:
            kw['accum_out'] = accum.ap
            writes.append(accum)
        return self._add('act', lambda e: e.activation(oa, ia, func, **kw), reads, writes, grp=grp)

    def tt_(self, eng, out, a, b, op, grp=None):
        oa, aa, ba = out.ap, a.ap, b.ap
        return self._add(eng, lambda e: e.tensor_tensor(oa, aa, ba, op), [a, b], [out], grp=grp)

    def ts(self, eng, out, a, s1, s2=None, op0=ALU.mult, op1=None, grp=None):
        oa, aa = out.ap, a.ap
        reads = [a]
        s1a = s1.ap if isinstance(s1, V) else float(s1)
        if isinstance(s1, V):
            reads.append(s1)
        s2a = None
        if s2 is not None:
            s2a = s2.ap if isinstance(s2, V) else float(s2)
            if isinstance(s2, V):
                reads.append(s2)
        if op1 is None:
            return self._add(eng, lambda e: e.tensor_scalar(oa, aa, s1a, None, op0), reads, [out], grp=grp)
        return self._add(eng, lambda e: e.tensor_scalar(oa, aa, s1a, s2a, op0, op1), reads, [out], grp=grp)

    def stt(self, eng, out, a, s, b, op0, op1, grp=None):
        oa, aa, ba = out.ap, a.ap, b.ap
        reads = [a, b]
        sa = s.ap if isinstance(s, V) else float(s)
        if isinstance(s, V):
            reads.append(s)
        return self._add(eng, lambda e: e.scalar_tensor_tensor(oa, aa, sa, ba, op0, op1), reads, [out], grp=grp)

    def copy(self, eng, out, a, grp=None):
        oa, aa = out.ap, a.ap
        if eng == 'act':
            return self._add(eng, lambda e: e.copy(oa, aa), [a], [out], grp=grp)
        return self._add(eng, lambda e: e.tensor_copy(oa, aa), [a], [out], grp=grp)

    def recip(self, out, a, grp=None):
        oa, aa = out.ap, a.ap
        return self._add('dve', lambda e: e.reciprocal(oa, aa), [a], [out], grp=grp)

    def scan(self, out, d0, d1, init, op0, op1, grp=None):
        oa, a0, a1 = out.ap, d0.ap, d1.ap
        return self._add('dve', lambda e: e.tensor_tensor_scan(oa, a0, a1, float(init), op0, op1),
                         [d0, d1], [out], grp=grp)

    def memset(self, eng, out, val):
        oa = out.ap
        return self._add(eng, lambda e: e.memset(oa, val), [], [out])

    def bn_stats(self, out, a):
        oa, aa = out.ap, a.ap
        return self._add('dve', lambda e: e.bn_stats(oa, aa), [a], [out])

    def bn_aggr(self, out, a):
        oa, aa = out.ap, a.ap
        return self._add('dve', lambda e: e.bn_aggr(oa, aa), [a], [out])

    def dump(self, name, v, shape, dt):
        o = self.dram(name, shape, dt, kind="ExternalOutput")
        self.dma(o.v(), v)
        self.outs.append(name)

    def finalize(self):
        nc = self.nc
        last_clk = {e: (0, 0, 0, 0) for e in ENGS + ('sp',)}
        known = {e: {} for e in ENGS + ('sp',)}
        for op in self.ops:
            E = op.eng
            clk = list(last_clk[E])
            waits = []
            for cid, (c, need) in op.chan_need.items():
                if c.wait_all:
                    need = -1
                    if known[E].get(cid) == -1:
                        continue
                elif known[E].get(cid, 0) >= need:
                    continue
                known[E][cid] = need
                waits.append(('c', c, need))
            cdeps = [d for d in op.deps if not d.is_dma]
            cdeps.sort(key=lambda d: -d.pos)
            for d in cdeps:
                di = EIDX[d.eng]
                if clk[di] >= d.pos:
                    continue
                if d.eng == E and E == 'pe':
                    continue
                waits.append(('e', d))
                d.needs_inc = True
                dc = d.done
                clk = [max(a, b) for a, b in zip(clk, dc)]
            op.waits = waits
            op.clk = tuple(clk)
            last_clk[E] = op.clk
            if not op.is_dma:
                dn = list(clk)
                dn[EIDX[E]] = max(dn[EIDX[E]], op.pos)
                op.done = tuple(dn)
        sems = {e: nc.alloc_semaphore(f"s_{e}") for e in ENGS}
        for c in self.chans:
            if c.count > 0:
                c.sem = nc.alloc_semaphore(f"c_{c.name}")
        for e in ENGS:
            n = 0
            for op in self.eng_ops[e]:
                if not op.is_dma and op.needs_inc:
                    n += 1
                    op.semval = n
            self.__dict__.setdefault('semmax', {})[e] = n

        def emit(eng, ename):
            for op in self.eng_ops[ename]:
                for w in op.waits:
                    if w[0] == 'c':
                        c, need = w[1], w[2]
                        eng.wait_ge(c.sem, 16 * (c.count if need == -1 else need))
                    else:
                        d = w[1]
                        eng.wait_ge(sems[d.eng], d.semval)
                ins = op.fn(eng)
                if op.is_dma:
                    ins.then_inc(op.chan.sem, 16)
                elif op.needs_inc:
                    ins.then_inc(sems[ename], 1)
            if ename == 'sp':
                for c in self.chans:
                    if c.count > 0:
                        eng.wait_ge(c.sem, 16 * c.count)

        with nc.Block() as block:
            @block.tensor
            def _(eng):
                emit(eng, 'pe')

            @block.scalar
            def _(eng):
                emit(eng, 'act')

            @block.vector
            def _(eng):
                emit(eng, 'dve')

            @block.gpsimd
            def _(eng):
                emit(eng, 'pool')

            @block.sync
            def _(eng):
                emit(eng, 'sp')
        return nc


def host_consts():
    p = np.arange(128)
    ident = np.eye(128, dtype=np.float32)
    d = p % 64
    perm = np.where((d % 32) < 16, p + 16, p - 16)
    permT = np.zeros((128, 128), np.float32)
    permT[perm, p] = 1.0
    ones = np.ones((128, 128), np.float32)
    startm = np.zeros((128, 512), np.float32)
    startm[:, ::32] = 1.0
    cmask = (p[:, None] // 32 == np.arange(4)[None, :]).astype(np.float32)
    constf = np.concatenate([ident, permT, ones, startm, cmask], axis=1)
    kk = p[:, None]
    qq = p[None, :]
    maskPrev = (qq <= kk).astype(np.float32)
    maskNext = (kk <= qq).astype(np.float32)
    same = (kk // 32) == (qq // 32)
    maskF = (same & (kk <= qq)).astype(np.float32)
    maskB = (same & (kk >= qq)).astype(np.float32)
    constb = np.concatenate([ident, maskPrev, maskNext, maskF, maskB], axis=1).astype(ml_dtypes.bfloat16)
    n = np.arange(SEQ)
    rows = (n // 64).astype(np.float32)
    cols = (n % 64).astype(np.float32)
    inv = (10000.0 ** (-np.arange(16, dtype=np.float32) / 16)).astype(np.float32)
    C = np.zeros((128, SEQ), np.float32)
    S = np.zeros((128, SEQ), np.float32)
    for pp in range(128):
        dd = pp % 64
        pos = rows if dd < 32 else cols
        ang = (pos * inv[dd % 16]).astype(np.float32)
        C[pp] = np.cos(ang)
        S[pp] = np.sin(ang) * (-1.0 if (dd % 32) < 16 else 1.0)
    rope = np.concatenate([C, S], axis=1).astype(np.float32)
    return constf, constb, rope


ARENA = 212000
O_CF, O_CB, O_SM, O_HT, O_W0, O_W1, O_U = 0, 3616, 4896, 8992, 45856, 58144, 70432
O_WK = O_U + 55296
O_MT = O_U + 73728
O_TR = O_U + 110592


def build(dbg=None, nseq=2, nlayer=DEPTH):
    dbg = dbg or set()
    P = Prog()
    nc = P.nc
    arena = nc.alloc_sbuf_tensor("arena", [128, ARENA], mybir.dt.uint8)

    class AT(TT):
        def __init__(self, name, off, shape, dt):
            self.name = name
            self.root = Buf(name)
            self.chan = None
            self.dram = False
            esz = 4 if dt == F32 else 2
            n = int(np.prod(shape[1:]))
            assert off + n * esz <= ARENA, (name, off, n * esz)
            ap = arena[0:shape[0], off:off + n * esz].bitcast(dt)
            if len(shape) == 3:
                ap = ap.rearrange("p (a b) -> p a b", b=shape[2])
            elif len(shape) == 4:
                ap = ap.rearrange("p (a b c) -> p a b c", b=shape[2], c=shape[3])
            self._ap = ap
            self.off = off
            self.nbytes = n * esz

        def full(self):
            return self._ap

    def DI(name, shape, dt=F32):
        return P.dram(name, shape, dt, kind="ExternalInput")

    x_in = DI("x", [2, SEQ, D]); c_in = DI("c", [2, D]); ctx_in = DI("ctx", [2, CTX, D]); cctx_in = DI("c_ctx", [1, D])
    w_ada = DI("w_ada", [DEPTH, D, 6 * D]); b_ada = DI("b_ada", [DEPTH, 6 * D])
    n1g = DI("norm1_g", [DEPTH, D]); n2g = DI("norm2_g", [DEPTH, D])
    w_in = DI("w_in", [DEPTH, D, D_IN]); sink_in = DI("attn_sink", [DEPTH, 8])
    lbl = DI("hg_lb_logits", [DEPTH * 2 * 4, 128]); hgg = DI("hg_norm_g", [DEPTH * 4, 128])
    vg_in = DI("mlp_v_norm_g", [DEPTH, 512]); vb_in = DI("mlp_v_norm_b", [DEPTH, 512])
    ws_in = DI("mlp_ws", [DEPTH, 4, 128, 128]); bs_in = DI("mlp_bs", [DEPTH, 4, 128])
    w_br = DI("w_branch", [DEPTH, 3, 512, D]); w_out = DI("w_out", [DEPTH, D, D])
    w_fg = DI("w_ffn_gate", [DEPTH, D, D_FF]); w_fu = DI("w_ffn_up", [DEPTH, D, D_FF]); w_fd = DI("w_ffn_down", [DEPTH, D_FF, D])
    fng = DI("final_norm_g", [1, D])
    constf_d = DI("constf", [128, 900]); constb_d = DI("constb", [128, 640], BF16); rope_d = DI("rope", [128, 4096])
    out_d = P.dram("out", [2, SEQ, D], F32, kind="ExternalOutput")
    modscr = P.dram("modscr", [DEPTH, 3, 6 * D], F32)
    xs_d = P.dram("xs", [T, D], F32)

    constf = AT("constf", O_CF, [128, 900], F32)
    constb = AT("constb", O_CB, [128, 640], BF16)
    ident_f = constf[:, 0:128]; permT = constf[:, 128:256]; ones_f = constf[:, 256:384]; startm = constf[:, 384:896]; cmask = constf[:, 896:900]
    ident_b = constb[:, 0:128]; maskPrev = constb[:, 128:256]; maskNext = constb[:, 256:384]
    maskF = constb[:, 384:512]; maskB = constb[:, 512:640]
    P.dma(constf.v(), constf_d.v())
    P.dma(constb.v(), constb_d.v())
    sm_off = [O_SM]

    def SM(name, shape, dt=F32):
        esz = 4 if dt == F32 else 2
        n = int(np.prod(shape[1:])) * esz
        n = (n + 31) // 32 * 32
        t = AT(name, sm_off[0], shape, dt)
        sm_off[0] += n
        assert sm_off[0] <= O_HT, name
        return t

    hT = AT("hT", O_HT, [128, KC, T], BF16)
    W = [AT("W0", O_W0, [128, 6144], BF16), AT("W1", O_W1, [128, 6144], BF16)]
    wrot = [0]
    PS = [P.psum(f"ps{i}", [128, 512], F32) for i in range(8)]

    def wload(src_ap_fn):
        w = W[wrot[0] % 2]
        wrot[0] += 1
        return w

    def wdma(dst, src):
        P.dma(dst, src, q='pool', hoist=True)

    def next_w():
        P.step_begin()
        w = W[wrot[0] % 2]
        wrot[0] += 1
        return w

    ACTIVE_TB_FULL = [(0, 512), (512, 512), (1024, 512), (1536, 512), (2048, 256)]
    ACTIVE_TB_LAT = [(256, 512), (768, 512), (1280, 512), (1792, 512)]

    def hkeys(t0, n):
        return [hT.k(t) for t in range(t0 // 128, (t0 + n) // 128)]

    def proj_fm(psv, wv, t0, n, hsrc=None):
        hs = hsrc or hT
        for kc in range(KC):
            rhs = V(hs.full()[:, kc, t0:t0 + n], hs.root, hs)
            P.mm(psv, wv[:, kc, :], rhs, start=(kc == 0), stop=(kc == KC - 1))

    small = {}
    ld = AT("ld0", O_U, [128, 128], F32)
    nrow = 0
    stage_rows = {}

    def stage(name, src_v, n):
        nonlocal nrow
        P.dma(V(ld.full()[nrow:nrow + n, :], ld.root, ld), src_v)
        stage_rows[name] = (nrow, n)
        nrow += n
    stage('n1g', n1g.v().rr("l (j p) -> (l j) p", p=128), 16)
    stage('n2g', n2g.v().rr("l (j p) -> (l j) p", p=128), 16)
    stage('lbl', lbl.v(), 16)
    stage('hgg', hgg.v(), 8)
    vecT = SM("vecT", [128, 64])
    psA = PS[0]
    P.tr(psA[:, 0:nrow], ld[0:nrow, :], ident_f[0:nrow, 0:nrow])
    P.copy('dve', vecT[:, 0:nrow], psA[:, 0:nrow])

    if 'r1' in dbg:
        return P

    def vcol(name, i):
        r0, n = stage_rows[name]
        return vecT[:, r0 + i:r0 + i + 1]

    def vcols(name, i0, n):
        r0, _ = stage_rows[name]
        return vecT[:, r0 + i0:r0 + i0 + n]
    oml = SM("oml", [128, 16]); noml = SM("noml", [128, 16]); onec = SM("onec", [128, 1])
    P.memset('dve', onec.v(), 1.0)
    P.memset('dve', oml[:, 0:8], 1.0)
    dl = SM("dl", [128, 8])
    P.tt_('dve', dl.v(), vcols('lbl', 0, 8), vcols('lbl', 8, 8), ALU.subtract)
    P.act(oml[:, 8:16], dl.v(), AF.Sigmoid)
    P.ts('dve', noml.v(), oml.v(), -1.0)
    esink = SM("esink", [128, 16])
    P.dma(esink.v(), sink_in.v().rr("l h -> (l h)").pbc(128))
    P.act(esink.v(), esink.v(), AF.Exp)

    if 'r2' in dbg:
        return P
    crow = AT("crow", O_U + 65536, [3, D], F32)
    P.dma(crow[0:2, :], c_in.v())
    P.dma(crow[2:3, :], cctx_in.v())
    P.act(crow.v(), crow.v(), AF.Silu)
    scT = SM("scT", [128, KC, 4], BF16)
    psB = PS[1]
    for kc in range(KC):
        P.tr(psB[:, kc * 4:kc * 4 + 3], crow[0:3, kc * 128:(kc + 1) * 128], ident_f[0:3, 0:3])
    P.copy('dve', scT[:, :, 0:3], psB[:, 0:32].rr("p (k r) -> p k r", r=4)[:, :, 0:3])
    if 'r3' in dbg:
        return P
    modrow = AT("modrow", O_U + 69632, [3, 6 * D], F32)
    bias3 = AT("bias3", O_U + 69632 + 24576, [3, 6 * D], F32)
    modT = {}
    P0_TENS = [ld, crow, modrow, bias3]
    for l in range(nlayer):
        P.dma(bias3.v(), b_ada[l, :].pbc(3))
        for cb in range(12):
            wb = next_w()
            wv = wb.v().rr("p (k n) -> p k n", k=KC)[:, :, 0:512]
            wdma(wv, w_ada[l].rr("(k p) n -> p k n", p=128)[:, :, cb * 512:(cb + 1) * 512])
            ps = PS[2 + cb % 2]
            for kc in range(KC):
                P.mm(ps[0:3, :], scT[:, kc, 0:3], wv[:, kc, :], start=(kc == 0), stop=(kc == KC - 1))
            P.tt_('dve', modrow[:, cb * 512:(cb + 1) * 512], ps[0:3, :], bias3[:, cb * 512:(cb + 1) * 512], ALU.add)
        P.dma(modscr[l], modrow.v())
        if 'r4' in dbg:
            return P
        for r in range(3):
            mld = AT(f"mld{l}{r}", O_U + 2048 * (l * 3 + r), [48, 128], F32)
            P0_TENS.append(mld)
            P.dma(mld.v(), modscr[l, r, :].rr("(j p) -> j p", p=128))
            if 'r5' in dbg:
                return P
            ps = PS[4 + r % 2]
            P.tr(ps[:, 0:48], mld.v(), ident_f[0:48, 0:48])
            if 'r6' in dbg:
                return P
            mt = SM(f"modT{l}{r}", [128, 32])
            P.copy('dve', mt[:, 0:16], ps[:, 0:16])
            P.copy('dve', mt[:, 16:32], ps[:, 24:40])
            P.stt('dve', mt[:, 8:16], mt[:, 8:16], 1.0, vcols('n1g', l * 8, 8), ALU.add, ALU.mult)
            P.stt('dve', mt[:, 24:32], mt[:, 24:32], 1.0, vcols('n2g', l * 8, 8), ALU.add, ALU.mult)
            modT[(l, r)] = mt
            if 'r7' in dbg:
                return P
        if 'r8' in dbg:
            return P

    ss_t = SM("ss_t", [128, 24]); rs_t = SM("rs_t", [128, 24])
    den_t = SM("den_t", [128, 8]); rec_t = SM("rec_t", [128, 8])
    xin = [AT(f"xin{i}", O_WK + 4096 * i, [128, D], F32) for i in range(2)]
    xsc = [AT(f"xsc{i}", O_WK + 8192 + 4096 * i, [128, D], F32) for i in range(2)]
    sqj = AT("sqj", O_WK + 16384, [128, D], BF16)
    xscN = [AT(f"xscN{i}", O_MT + 4096 * i, [128, D], F32) for i in range(2)]
    sqjN = AT("sqjN", O_MT + 8192, [128, D], BF16)
    aT = AT("aT", O_U, [128, 4, T], BF16); bT = AT("bT", O_U + 18432, [128, 4, T], BF16); cT = AT("cT", O_U + 36864, [128, 4, T], BF16)
    mT = AT("mT", O_MT, [128, KC, T], BF16)
    X = AT("X", O_U, [128, NT, D], F32)
    ropeT = AT("ropeT", O_WK, [128, 4096], F32)
    qT = AT("qT", O_WK + 16384, [128, 4, T], BF16)
    kTg = [AT(f"kT{g}", O_WK + 34816 + 4608 * g, [128, T], BF16) for g in range(2)]
    kz = [[AT(f"kz{g}{a}", O_WK + 62080 + 4608 * (2 * g + a), [128, T], BF16) for a in range(2)] for g in range(2)]
    v_sb = AT("v_sb", O_WK + 44032, [128, NT, 2, 65], BF16)
    qf32 = [AT(f"qf32{i}", O_WK + 48768 + 2048 * i, [128, 512], F32) for i in range(2)]
    rt1 = AT("rt1", O_WK + 52864, [128, 512], F32); rt2 = AT("rt2", O_WK + 54912, [128, 512], F32)
    pTb = [AT(f"pT{i}", O_WK + 56960 + 1024 * i, [128, 4, 128], BF16) for i in range(3)]
    a_tok = [AT(f"a_tok{i}", O_WK + 60032 + 1024 * i, [128, 8, 64], BF16) for i in range(2)]
    cnt = {'x': 0, 'q': 0, 'p': 0, 'a': 0, 'ps': 0}

    def xsrc(s, l, t):
        if l == 0:
            if t < 2:
                return ctx_in[s, t * 128:(t + 1) * 128, :]
            return x_in[s, (t - 2) * 128:(t - 1) * 128, :]
        return xs_d[t * 128:(t + 1) * 128, :]

    def norm_stats(xv, t, sq_t):
        P.act(sq_t.v(), xv, AF.Square, accum=ss_t[:, t:t + 1], grp=('only', ss_t.root, 'ss'))

    def norm_rstd(t_lo, t_hi):
        P.act(rs_t[:, t_lo:t_hi], ss_t[:, t_lo:t_hi], AF.Sqrt, bias=epsc.v(), scale=1.0 / D)
        P.recip(rs_t[:, t_lo:t_hi], rs_t[:, t_lo:t_hi])

    def norm_apply(xv, t, mt, c0, xs):
        P.act(xs.v(), xv, AF.Copy, scale=rs_t[:, t:t + 1])
        for half in range(2):
            ps = PS[(t % 2) * 2 + half]
            for q in range(4):
                kc = half * 4 + q
                P.tr(ps[:, q * 128:(q + 1) * 128], xs[:, kc * 128:(kc + 1) * 128], ident_f)
            for q in range(4):
                kc = half * 4 + q
                dst = V(hT.full()[:, kc, t * 128:(t + 1) * 128], hT.k(t).buf, hT)
                if t % 2 == 0:
                    P.ts('dve', dst, ps[:, q * 128:(q + 1) * 128], mt[:, c0 + 8 + kc:c0 + 9 + kc], mt[:, c0 + kc:c0 + kc + 1],
                         ALU.mult, ALU.add, grp=('h', t))
                else:
                    P.act(dst, ps[:, q * 128:(q + 1) * 128], AF.Identity, bias=mt[:, c0 + kc:c0 + kc + 1],
                          scale=mt[:, c0 + 8 + kc:c0 + 9 + kc], grp=('h', t))

    epsc = SM("epsc", [128, 1])
    P.memset('dve', epsc.v(), EPS)
    negc = SM("negc", [128, 1])
    P.memset('dve', negc.v(), -1.0)

    def hslice(kc, t0, n):
        return V(hT.full()[:, kc, t0:t0 + n], hT.root, hT)

    def phaseA(s, l, need_ctx):
        for tn_ in A_TENS:
            P.alias(tn_, [X, xscN[0], xscN[1], sqjN])
        P.dma(ropeT.v(), rope_d.v())
        wl = w_in[l].rr("(k p) n -> p k n", p=128)
        wb = next_w()
        wv = wb.v()[:, 0:KC * 512].rr("p (k n) -> p k n", k=KC)
        wdma(wv, wl[:, :, 0:512])
        tbs = ACTIVE_TB_FULL if need_ctx else ACTIVE_TB_LAT

        def rope_block(ps, dstv, t0, n):
            i = cnt['q']; cnt['q'] += 1
            if t0 < CTX:
                nc_ = min(n, CTX - t0)
                P.copy('act', dstv[:, 0:nc_], ps[:, 0:nc_])
                if nc_ == n:
                    return
                a0 = nc_
            else:
                a0 = 0
            m = n - a0
            l0 = t0 + a0 - CTX
            qf = qf32[i % 2]
            P.copy('act', qf[:, 0:m], ps[:, a0:n])
            pp = PS[4 + i % 2]
            P.mm(pp[:, 0:m], permT, qf[:, 0:m])
            P.tt_('dve', rt1[:, 0:m], qf[:, 0:m], ropeT[:, l0:l0 + m], ALU.mult)
            P.tt_('dve', rt2[:, 0:m], pp[:, 0:m], ropeT[:, 2048 + l0:2048 + l0 + m], ALU.mult)
            P.tt_('dve', dstv[:, a0:n], rt1[:, 0:m], rt2[:, 0:m], ALU.add)

        def run_jobs(jobs):
            def proj(job, idx):
                wsl, dstv, t0, n = job
                ps = PS[2 + idx % 2]
                proj_fm(ps[:, 0:n], wsl, t0, n)
                return ps
            pend = proj(jobs[0], 0)
            for i, job in enumerate(jobs):
                nxt = proj(jobs[i + 1], i + 1) if i + 1 < len(jobs) else None
                rope_block(pend, job[1], job[2], job[3])
                pend = nxt

        run_jobs([(wv[:, :, j * 128:(j + 1) * 128], V(qT.full()[:, j, t0:t0 + n], qT.root, qT), t0, n)
                  for j in range(4) for (t0, n) in tbs])
        wb2 = next_w()
        wkv = wb2.v()[:, 0:KC * 384].rr("p (k n) -> p k n", k=KC)
        wk = wkv[:, :, 0:256]
        for g in range(2):
            for dup in range(2):
                wdma(wk[:, :, (g * 2 + dup) * 64:(g * 2 + dup + 1) * 64], wl[:, :, 512 + g * 64:512 + (g + 1) * 64])
        wdma(wkv[:, :, 256:384], wl[:, :, 640:768])
        run_jobs([(wk[:, :, g * 128:(g + 1) * 128], kTg[g][:, t0:t0 + n], t0, n)
                  for g in range(2) for (t0, n) in ACTIVE_TB_FULL])
        for g in range(2):
            for a in range(2):
                P.memset('dve', kz[g][a][(1 - a) * 64:(2 - a) * 64, :], 0.0)
                P.copy('dve' if a == 0 else 'act', kz[g][a][a * 64:(a + 1) * 64, :], kTg[g][a * 64:(a + 1) * 64, :])
        P.memset('dve', v_sb.v(), 1.0)
        for t in range(NT):
            ps = PS[2 + cnt['ps'] % 2]; cnt['ps'] += 1
            for kc in range(KC):
                P.mm(ps[:, 0:128], hslice(kc, t * 128, 128), wkv[:, kc, 256:384], start=(kc == 0), stop=(kc == KC - 1))
            P.copy('dve', v_sb[:, t, :, 0:64], ps[:, 0:128].rr("p (g d) -> p g d", g=2))
        if 'a2' in dbg:
            return
        blocks = []
        if need_ctx:
            for qt in range(2):
                blocks.append((qt, [(0, None), (1, None)]))
        for j in range(16):
            ks = []
            if j > 0:
                ks.append((2 + j - 1, maskPrev))
            ks.append((2 + j, None))
            if j < 15:
                ks.append((2 + j + 1, maskNext))
            ks += [(0, None), (1, None)]
            blocks.append((2 + j, ks))
        items = []
        for bi, (qt, ks) in enumerate(blocks):
            for g in range(2):
                for ci, (kt, mask) in enumerate(ks):
                    items.append((bi, qt, g, ci, kt, mask, len(ks)))

        def score(item, idx):
            bi, qt, g, ci, kt, mask, nks = item
            pss = PS[idx % 2]
            for a in range(2):
                P.mm(pss[:, a * 256:(a + 1) * 256].rr("p (b q) -> p b q", b=2),
                     kz[g][a][:, kt * 128:(kt + 1) * 128],
                     V(qT.full()[:, 2 * g:2 * g + 2, qt * 128:(qt + 1) * 128], qT.root, qT))
            return pss

        def rest(item, pss):
            bi, qt, g, ci, kt, mask, nks = item
            at = a_tok[bi % 2]
            po = PS[6 + g]
            pov = po[:, 0:260].rr("p (s d) -> p s d", d=65)
            pt = pTb[cnt['p'] % 3]; cnt['p'] += 1
            P.act(pt.v().rr("p s q -> p (s q)"), pss.v(), AF.Exp, scale=0.125)
            if mask is not None:
                P.tt_('dve', pt.v(), pt.v(), mask.unsq(1).bc([128, 4, 128]), ALU.mult)
            for sl in range(4):
                P.mm(pov[:, sl, :], pt[:, sl, :], v_sb[:, kt, g, :], start=(ci == 0 and sl == 0), stop=(ci == nks - 1 and sl == 3))
            if ci != nks - 1:
                return
            es = esink[:, l * 8 + 4 * g:l * 8 + 4 * g + 4].rr("p (b a) -> p a b", a=2)
            dn = den_t[:, 4 * g:4 * g + 4]
            P.tt_('dve', dn.rr("p (a b) -> p a b", a=2), pov[:, :, 64].rr("p (a b) -> p a b", a=2), es, ALU.add)
            P.recip(rec_t[:, 4 * g:4 * g + 4], dn)
            dst = at[:, 4 * g:4 * g + 4, :].rr("p (b a) d -> p a b d", a=2)
            P.tt_('dve', dst, pov[:, :, 0:64].rr("p (a b) d -> p a b d", a=2),
                  rec_t[:, 4 * g:4 * g + 4].rr("p (a b) -> p a b", a=2).unsq(3).bc([128, 2, 2, 64]), ALU.mult)
            if g != 1:
                return
            pt2 = PS[4 + bi % 2]
            ptb = pt2.v().bitcast(BF16)
            for jq in range(4):
                P.tr(ptb[:, jq * 128:(jq + 1) * 128], at[:, 2 * jq:2 * jq + 2, :].rr("p h d -> p (h d)"), ident_b)
            P.copy('act', V(aT.full()[:, :, qt * 128:(qt + 1) * 128], aT.root, aT), ptb[:, 0:512].rr("p (j q) -> p j q", j=4))

        pending = score(items[0], 0)
        for i, item in enumerate(items):
            nxt = score(items[i + 1], i + 1) if i + 1 < len(items) else None
            rest(item, pending)
            pending = nxt

    OB = O_WK + 64
    A_TENS = [ropeT, qT, kTg[0], kTg[1], kz[0][0], kz[0][1], kz[1][0], kz[1][1], v_sb, qf32[0], qf32[1], rt1, rt2] + pTb + a_tok
    QF, KF, KH, QB, QH, KB = [AT(nm, OB + 4608 * i, [128, T], BF16) for i, nm in enumerate(["QF", "KF", "KH", "QB", "QH", "KB"])]
    v_tok = AT("v_tok", OB + 27648, [128, NT, 128], BF16)
    gs = AT("gs", OB + 32256, [128, T], F32)
    o_acc = AT("o_acc", OB + 41472, [128, T], F32)
    tmpB = [AT(f"tmpB{i}", OB + 50688 + 2048 * i, [128, 512], F32) for i in range(8)]
    tmpB.append(AT("tmpB_fbs", OB + 50688 + 2048 * 8, [128, 528], F32))
    tmpB[3] = AT("tmpB_ffx", OB + 50688 + 2048 * 8 + 2112, [128, 528], F32)
    OC = O_U + 36864
    tmpB2 = [AT(f"tmpC{i}", OC + 2048 * i, [128, 512], F32) for i in range(8)]
    tmpB2[3] = AT("tmpC_ffx", OC + 2048 * 3, [128, 513], F32)
    for i_ in range(4, 8):
        tmpB2[i_] = AT(f"tmpC{i_}", OC + 2052 + 2048 * (i_ - 1), [128, 512], F32)
    tmpB2.append(AT("tmpC_fbs", OC + 2052 + 2048 * 7, [128, 513], F32))
    att_sb = [AT(f"att_sb{i}", OB + 71296 + 256 * i, [128, 128], BF16) for i in range(4)]
    kt_tok = [AT(f"kt_tok{i}", OB + 82112 + 1024 * i, [128, 4, 128], BF16) for i in range(4)]
    S_f = [[AT(f"S_f{d}{r}", OB + 78464 + 512 * (3 * d + r), [128, 128], F32) for r in range(3)] for d in range(2)]
    S_b = [[AT(f"S_b{d}{r}", OB + 72320 + 256 * (12 * d + r), [128, 128], BF16) for r in range(12)] for d in range(2)]
    EF = AT("EF", OB + 81536, [128, 72], F32); EB = AT("EB", OB + 81824, [128, 72], F32)
    iEF = AT("iEF", OB + 56832, [128, 72], F32); iEB = AT("iEB", OB + 57120, [128, 72], F32)
    B_TENS = [QF, KF, KH, QB, QH, KB, v_tok, gs, o_acc] + tmpB + tmpB2 + att_sb + kt_tok + S_f[0] + S_f[1] + S_b[0] + S_b[1] + [EF, EB, iEF, iEB]
    QSC = 128.0 ** -0.5

    def phaseB(s, l, need_ctx):
        for tnew in B_TENS:
            P.alias(tnew, A_TENS + [cT])
        wl = w_in[l].rr("(k p) n -> p k n", p=128)
        for tset in (tmpB, tmpB2):
            P.memset('dve', tset[8][:, 0:1], 1.0)
            P.memset('dve', tset[3].v(), 1.0)
        for hh in range(4):
            mark(f's{s}l{l}B{hh}p')
            wb = next_w()
            wv5 = wb.v()[:, 0:KC * 5 * 128].rr("p (k a n) -> p k a n", k=KC, a=5)
            for a in range(5):
                wdma(wv5[:, :, a, :], wl[:, :, 768 + a * 512 + hh * 128:768 + a * 512 + (hh + 1) * 128])
            omf = oml[:, l * 8 + hh:l * 8 + hh + 1]; nomf = noml[:, l * 8 + hh:l * 8 + hh + 1]
            omb = oml[:, l * 8 + 4 + hh:l * 8 + 5 + hh]; nomb = noml[:, l * 8 + 4 + hh:l * 8 + 5 + hh]
            for tbi, (t0, n) in enumerate(ACTIVE_TB_FULL):
                tq, sg, kf, ff, rf, ri, rho, tg, fbs = (tmpB if tbi % 2 == 0 else tmpB2)
                rib = ri; sgb = sg; kb = ff
                c0 = t0 // 32; ncn = n // 32
                ps = PS[cnt['ps'] % 4]; cnt['ps'] += 1
                proj_fm(ps[:, 0:n], wv5[:, :, 0, :], t0, n)
                P.act(tq[:, 0:n], ps[:, 0:n], AF.Silu)
                ps = PS[cnt['ps'] % 4]; cnt['ps'] += 1
                proj_fm(ps[:, 0:n], wv5[:, :, 1, :], t0, n)
                P.act(sg[:, 0:n], ps[:, 0:n], AF.Sigmoid, scale=-1.0)
                P.act(kf[:, 0:n], sg[:, 0:n], AF.Copy, scale=omf)
                P.act(ff[:, 0:n], sg[:, 0:n], AF.Identity, bias=onec.v(), scale=nomf)
                P.scan(rf[:, 0:n], startm[:, 0:n], ff[:, 0:n], 0.0, ALU.max, ALU.mult)
                P.scan(ri[:, 0:n][:, ::-1], ff[:, 1:n + 1][:, ::-1], startm[:, 0:n], 0.0, ALU.mult, ALU.max)
                P.stt('dve', QF[:, t0:t0 + n], tq[:, 0:n], QSC, rf[:, 0:n], ALU.mult, ALU.mult)
                P.copy('dve', EF[:, c0:c0 + ncn], rf[:, 0:n].rr("p (c t) -> p c t", t=32)[:, :, 31])
                P.tt_('pool', KH[:, t0:t0 + n], kf[:, 0:n], ri[:, 0:n], ALU.mult)
                ps = PS[cnt['ps'] % 4]; cnt['ps'] += 1
                proj_fm(ps[:, 0:n], wv5[:, :, 2, :], t0, n)
                P.act(sgb[:, 0:n], ps[:, 0:n], AF.Sigmoid, scale=-1.0)
                P.act(kb[:, 0:n], sgb[:, 0:n], AF.Copy, scale=omb)
                P.act(fbs[:, 1:n + 1], sgb[:, 0:n], AF.Identity, bias=onec.v(), scale=nomb)
                P.scan(rho[:, 0:n], fbs[:, 0:n], startm[:, 0:n], 0.0, ALU.mult, ALU.max)
                P.tt_('dve', KB[:, t0:t0 + n], kb[:, 0:n], rho[:, 0:n], ALU.mult)
                P.scan(rib[:, 0:n][:, ::-1], startm[:, 0:n], fbs[:, 1:n + 1][:, ::-1], 0.0, ALU.max, ALU.mult)
                P.copy('dve', EB[:, c0:c0 + ncn], rib[:, 0:n].rr("p (c t) -> p c t", t=32)[:, :, 0])
                P.stt('dve', QH[:, t0:t0 + n], tq[:, 0:n], QSC, rib[:, 0:n], ALU.mult, ALU.mult)
                ps = PS[cnt['ps'] % 4]; cnt['ps'] += 1
                proj_fm(ps[:, 0:n], wv5[:, :, 4, :], t0, n)
                P.act(tg[:, 0:n], ps[:, 0:n], AF.Silu)
                P.ts('dve', gs[:, t0:t0 + n], tg[:, 0:n], vcol('hgg', l * 4 + hh))
            P.recip(iEF.v(), EF.v())
            P.recip(iEB.v(), EB.v())
            for (t0, n) in ACTIVE_TB_FULL:
                c0 = t0 // 32; ncn = n // 32
                P.tt_('pool', KF[:, t0:t0 + n].rr("p (c t) -> p c t", t=32), KH[:, t0:t0 + n].rr("p (c t) -> p c t", t=32),
                      iEF[:, c0:c0 + ncn].unsq(2).bc([128, ncn, 32]), ALU.mult)
                P.tt_('pool', QB[:, t0:t0 + n].rr("p (c t) -> p c t", t=32), QH[:, t0:t0 + n].rr("p (c t) -> p c t", t=32),
                      iEB[:, c0:c0 + ncn].unsq(2).bc([128, ncn, 32]), ALU.mult)
            mark(f's{s}l{l}B{hh}v')
            for t in range(NT):
                ps = PS[2 + cnt['ps'] % 2]; cnt['ps'] += 1
                for kc in range(KC):
                    P.mm(ps[:, 0:128], hslice(kc, t * 128, 128), wv5[:, kc, 3, :], start=(kc == 0), stop=(kc == KC - 1))
                P.copy('act', v_tok[:, t, :], ps[:, 0:128])
            mark(f's{s}l{l}B{hh}r')
            P.memset('dve', o_acc.v(), 0.0)
            for d in range(2):
                P.memset('dve', S_f[d][0].v(), 0.0)
                P.memset('dve', S_b[d][0].v(), 0.0)
            border = [1, 0] + list(range(17, 1, -1))
            step = [0, 0]

            def cfg(d, i):
                t = i if d == 0 else border[i]
                if d == 0:
                    return t, QF, KF, KH, QF, EF, maskF, [0, 1, 2, 3]
                return t, QB, KB, KB, QH, EB, maskB, [3, 2, 1, 0]

            def stage1(i):
                for d in range(2):
                    t, Qa, Ka, Kkv, Qint, E, mask, order = cfg(d, i)
                    sl = slice(t * 128, (t + 1) * 128)
                    pa = PS[2]
                    P.mm(pa[:, d * 128:(d + 1) * 128], Ka[:, sl], Qa[:, sl])
                    att = att_sb[2 * d + i % 2]
                    P.tt_('dve', att.v(), pa[:, d * 128:(d + 1) * 128], mask, ALU.mult)
                    ptk = PS[3].v().bitcast(BF16)
                    P.tr(ptk[:, d * 128:(d + 1) * 128], Kkv[:, sl], ident_b)
                    kt = kt_tok[2 * d + i % 2]
                    for c in range(4):
                        P.act(kt[:, c, :], ptk[:, d * 128:(d + 1) * 128], AF.Copy, scale=cmask[:, c:c + 1])
                for d in range(2):
                    t, Qa, Ka, Kkv, Qint, E, mask, order = cfg(d, i)
                    kt = kt_tok[2 * d + i % 2]
                    pk = PS[[0, 1, 6, 7][2 * d + i % 2]]
                    for c in range(4):
                        P.mm(pk[:, c * 128:(c + 1) * 128], kt[:, c, :], v_tok[:, t, :])
                for ci in range(4):
                    for d in range(2):
                        t, Qa, Ka, Kkv, Qint, E, mask, order = cfg(d, i)
                        c = order[ci]
                        pk = PS[[0, 1, 6, 7][2 * d + i % 2]]
                        gc = t * 4 + c
                        k = step[d]
                        P.stt('dve', S_f[d][(k + 1) % 3].v(), S_f[d][k % 3].v(), E[:, gc:gc + 1], pk[:, c * 128:(c + 1) * 128], ALU.mult, ALU.add)
                        P.copy('dve' if k % 4 == 3 else 'pool', S_b[d][(k + 1) % 12].v(), S_f[d][(k + 1) % 3].v())
                        step[d] += 1

            def stage2(i):
                for d in range(2):
                    t, Qa, Ka, Kkv, Qint, E, mask, order = cfg(d, i)
                    sl = slice(t * 128, (t + 1) * 128)
                    att = att_sb[2 * d + i % 2]
                    po = PS[4 + d]
                    P.mm(po[:, 0:128], v_tok[:, t, :], att.v(), start=True, stop=False)
                    k0 = i * 4
                    for ci, c in enumerate(order):
                        cs = slice(t * 128 + c * 32, t * 128 + c * 32 + 32)
                        P.mm(po[:, c * 32:(c + 1) * 32], S_b[d][(k0 + ci) % 12].v(), Qint[:, cs], start=False, stop=(ci == 3))
                    P.tt_('dve', o_acc[:, sl], o_acc[:, sl], po[:, 0:128], ALU.add)

            for i in range(NT + 1):
                if i < NT:
                    stage1(i)
                if i >= 1:
                    stage2(i - 1)
            mark(f's{s}l{l}B{hh}o')
            tq, sg, kf = tmpB[0], tmpB[1], tmpB[2]
            for (t0, n) in (ACTIVE_TB_FULL if need_ctx else ACTIVE_TB_LAT):
                P.act(tq[:, 0:n], o_acc[:, t0:t0 + n], AF.Square)
                ps = PS[cnt['ps'] % 2]; cnt['ps'] += 1
                P.mm(ps[:, 0:n], ones_f, tq[:, 0:n])
                P.act(sg[:, 0:n], ps[:, 0:n], AF.Sqrt, bias=epsc.v(), scale=1.0 / 128)
                P.recip(sg[:, 0:n], sg[:, 0:n])
                P.tt_('dve', kf[:, 0:n], o_acc[:, t0:t0 + n], sg[:, 0:n], ALU.mult)
                P.tt_('dve', V(bT.full()[:, hh, t0:t0 + n], bT.root, bT), kf[:, 0:n], gs[:, t0:t0 + n], ALU.mult)
            if 'b' in dbg and s == 0 and hh == 0:
                P.dump(f"d_oacc{l}", o_acc.v(), [128, T], F32)

    vn_tok = AT("vn_tok", O_WK, [128, NT, 512], BF16)
    cvf = [AT(f"cvf{i}", O_WK + 18432 + 2048 * i, [128, 512], F32) for i in range(2)]
    s_sb = [AT(f"s_sb{i}", O_WK + 22528 + 2048 * i, [128, 512], F32) for i in range(2)]
    wsT = AT("wsT", O_WK + 26624, [128, 4, 128], BF16); ws_raw = AT("ws_raw", O_WK + 27648, [128, 4, 128], F32)
    bs_b = AT("bs_b", O_WK + 29696, [128, 4, 128], F32)
    vg_b = AT("vg_b", O_WK + 31744, [128, 512], F32); vb_b = AT("vb_b", O_WK + 33792, [128, 512], F32)
    smC = SM("smC", [128, 8])
    C_TENS = [vn_tok] + cvf + s_sb + [wsT, ws_raw, bs_b, vg_b, vb_b]

    def phaseC(s, l, need_ctx):
        for tnew in C_TENS:
            P.alias(tnew, B_TENS)
        P.alias(cT, tmpB2)
        wl = w_in[l].rr("(k p) n -> p k n", p=128)
        P.dma(ws_raw.v(), ws_in[l].rr("g t s -> t g s"))
        P.dma(bs_b.v().rr("p g t -> p (g t)"), bs_in[l].rr("g t -> (g t)").pbc(128))
        P.dma(vg_b.v(), vg_in[l, :].pbc(128))
        P.dma(vb_b.v(), vb_in[l, :].pbc(128))
        ps = PS[2]
        for g in range(4):
            P.tr(ps[:, g * 128:(g + 1) * 128], ws_raw[:, g, :], ident_f)
        P.copy('dve', wsT.v().rr("p g t -> p (g t)"), ps.v())
        wb = next_w()
        wcv = wb.v()[:, 0:KC * 512].rr("p (k n) -> p k n", k=KC)
        wdma(wcv, wl[:, :, 3840:4352])
        t_lo = 0 if need_ctx else 2
        for t in range(t_lo, NT):
            ps = PS[cnt['ps'] % 2]; cnt['ps'] += 1
            for kc in range(KC):
                P.mm(ps.v(), hslice(kc, t * 128, 128), wcv[:, kc, :], start=(kc == 0), stop=(kc == KC - 1))
            cv = cvf[t % 2]; jk = s_sb[t % 2]
            P.act(cv.v(), ps.v(), AF.Copy, accum=smC[:, 0:1])
            P.ts('dve', smC[:, 1:2], smC[:, 0:1], -1.0 / 512)
            P.act(jk.v(), cv.v(), AF.Square, bias=smC[:, 1:2], accum=smC[:, 2:3])
            P.act(smC[:, 3:4], smC[:, 2:3], AF.Sqrt, bias=epsc.v(), scale=1.0 / 512)
            P.recip(smC[:, 3:4], smC[:, 3:4])
            P.ts('dve', cv.v(), cv.v(), smC[:, 1:2], smC[:, 3:4], ALU.add, ALU.mult)
            P.tt_('dve', cv.v(), cv.v(), vg_b.v(), ALU.mult)
            P.tt_('dve', vn_tok[:, t, :], cv.v(), vb_b.v(), ALU.add)
        wb = next_w()
        wcu = wb.v()[:, 0:KC * 512].rr("p (k n) -> p k n", k=KC)
        wdma(wcu, wl[:, :, 3328:3840])
        for g in range(4):
            for (t0, n) in (ACTIVE_TB_FULL if need_ctx else ACTIVE_TB_LAT):
                ps = PS[cnt['ps'] % 2]; cnt['ps'] += 1
                proj_fm(ps[:, 0:n], wcu[:, :, g * 128:(g + 1) * 128], t0, n)
                p2 = PS[2 + cnt['ps'] % 2]
                nt_ = n // 128
                for i in range(nt_):
                    P.mm(p2[:, i * 128:(i + 1) * 128], vn_tok[:, t0 // 128 + i, g * 128:(g + 1) * 128], wsT[:, g, :])
                sb = s_sb[cnt['ps'] % 2]
                P.tt_('dve', sb[:, 0:n].rr("p (i t) -> p i t", t=128), p2[:, 0:n].rr("p (i t) -> p i t", t=128),
                      bs_b[:, g, :].unsq(1).bc([128, nt_, 128]), ALU.add)
                P.tt_('dve', V(cT.full()[:, g, t0:t0 + n], cT.root, cT), ps[:, 0:n], sb[:, 0:n], ALU.mult)

    sgD2 = [[AT(f"sgD{k}{i}", O_WK + (0 if k == 0 else 55296) + 2048 * i, [128, 512], F32) for i in range(3)] for k in range(2)]
    tmD2 = [[AT(f"tmD{k}{i}", O_WK + (0 if k == 0 else 55296) + 6144 + 2048 * i, [128, 512], F32) for i in range(3)] for k in range(2)]
    D_TENS = sgD2[0] + sgD2[1] + tmD2[0] + tmD2[1]

    def phaseD(s, l, need_ctx):
        for tnew in D_TENS + [mT]:
            P.alias(tnew, C_TENS + B_TENS)
        wl = w_in[l].rr("(k p) n -> p k n", p=128)
        wbl = w_br[l].rr("i (k p) n -> p k i n", p=128)
        brs = [aT, bT, cT]
        for nb in range(8):
            wb = next_w()
            wg3 = wb.v()[:, 0:KC * 384].rr("p (k i n) -> p k i n", k=KC, i=3)
            wbr = wb.v()[:, KC * 384:KC * 384 + 4 * 384].rr("p (k i n) -> p k i n", k=4, i=3)
            for i in range(3):
                wdma(wg3[:, :, i, :], wl[:, :, 4352 + i * 1024 + nb * 128:4352 + i * 1024 + (nb + 1) * 128])
                wdma(wbr[:, :, i, :], wbl[:, :, i, nb * 128:(nb + 1) * 128])
            for (t0, n) in (ACTIVE_TB_FULL if need_ctx else ACTIVE_TB_LAT):
                sgD = sgD2[cnt['a'] % 2]; tmD = tmD2[cnt['a'] % 2]; cnt['a'] += 1
                for i in range(3):
                    ps = PS[cnt['ps'] % 2]; cnt['ps'] += 1
                    proj_fm(ps[:, 0:n], wg3[:, :, i, :], t0, n)
                    P.act(sgD[i][:, 0:n], ps[:, 0:n], AF.Sigmoid)
                    p2 = PS[2 + cnt['ps'] % 2]
                    for kc in range(4):
                        P.mm(p2[:, 0:n], wbr[:, kc, i, :], V(brs[i].full()[:, kc, t0:t0 + n], brs[i].root, brs[i]),
                             start=(kc == 0), stop=(kc == 3))
                    P.tt_('dve', tmD[i][:, 0:n], sgD[i][:, 0:n], p2[:, 0:n], ALU.mult)
                P.tt_('pool', tmD[0][:, 0:n], tmD[0][:, 0:n], tmD[1][:, 0:n], ALU.add)
                P.tt_('pool', V(mT.full()[:, nb, t0:t0 + n], mT.root, mT), tmD[0][:, 0:n], tmD[2][:, 0:n], ALU.add)

    g1b = AT("g1b", O_TR, [128, D], F32); cg1b = AT("cg1b", O_TR + 4096, [128, D], F32)
    g2b = AT("g2b", O_TR + 8192, [128, D], F32); cg2b = AT("cg2b", O_TR + 12288, [128, D], F32)
    tmpE = [AT(f"tmpE{i}", O_TR + 16384 + 2048 * i, [128, 512], F32) for i in range(2)]
    xsR = AT("xsR", O_TR + 20480, [128, D], F32); sqjR = AT("sqjR", O_TR + 24576, [128, D], BF16)
    sgt = [AT(f"sgt{i}", O_TR + 26624 + 2048 * i, [128, 512], F32) for i in range(2)]
    xsR2 = AT("xsR2", O_TR + 26624, [128, D], F32)
    uT = AT("uT", O_MT, [128, KC, T], BF16)
    R_TENS = [g1b, cg1b, g2b, cg2b, xsR, sqjR] + tmpE + sgt
    xch = [P.new_chan(f"X{t}") for t in range(NT)]
    M_ALL = A_TENS + B_TENS + C_TENS + D_TENS + [aT, bT, cT, xscN[0], xscN[1], sqjN]

    def resid_add(psv, gb, xv, k):
        tm = tmpE[k % 2]
        P.tt_('dve', tm.v(), psv, gb, ALU.mult)
        P.tt_('pool', xv, xv, tm.v(), ALU.add)

    def phaseEFG(s, l, need_ctx, last):
        for tnew in R_TENS + [X]:
            P.alias(tnew, M_ALL)
        t_lo = 0 if need_ctx else 2
        P.dma(g1b.v(), modscr[l, s, 2048:3072].pbc(128))
        P.dma(g2b.v(), modscr[l, s, 5120:6144].pbc(128))
        if need_ctx:
            P.dma(cg1b.v(), modscr[l, 2, 2048:3072].pbc(128))
            P.dma(cg2b.v(), modscr[l, 2, 5120:6144].pbc(128))
        k = 0
        for h in range(2):
            wb = next_w()
            wov = wb.v()[:, 0:KC * 512].rr("p (k n) -> p k n", k=KC)
            wdma(wov, w_out[l].rr("(k p) n -> p k n", p=128)[:, :, h * 512:(h + 1) * 512])
            for t in range(t_lo, NT):
                xv = V(X.full()[:, t, :], X.k(t).buf, X)
                if h == 0:
                    P.dma(xv, xsrc(s, l, t), chan=xch[t])
                ps = PS[cnt['ps'] % 2]; cnt['ps'] += 1
                for kc in range(KC):
                    P.mm(ps.v(), V(mT.full()[:, kc, t * 128:(t + 1) * 128], mT.root, mT), wov[:, kc, :],
                         start=(kc == 0), stop=(kc == KC - 1))
                gb = (cg1b if t < 2 else g1b)[:, h * 512:(h + 1) * 512]
                resid_add(ps.v(), gb, xv[:, h * 512:(h + 1) * 512], k); k += 1
        if 'xm' in dbg and s == 0:
            P.dump(f"d_xm{l}", X.v(), [128, NT, D], F32)
        mark(f's{s}l{l}F')
        P.alias(xsR2, sgt)
        for t in range(t_lo, NT):
            norm_stats(V(X.full()[:, t, :], X.k(t).buf, X), t, sqjR)
        norm_rstd(t_lo, NT)
        for t in range(t_lo, NT):
            norm_apply(V(X.full()[:, t, :], X.k(t).buf, X), t, modT[(l, 2 if t < 2 else s)], 16, [xsR, xsR2][t % 2])
        for sg_ in sgt:
            P.alias(sg_, [xsR2])
        mark(f's{s}l{l}G')
        P.alias(uT, [mT])
        tbs = ACTIVE_TB_FULL if need_ctx else ACTIVE_TB_LAT
        for (j0, nj) in [(0, 8), (8, 7), (15, 7)]:
            for jl in range(nj):
                jf = j0 + jl
                wb = next_w()
                wv = wb.v()[:, 0:KC * 256].rr("p (k a n) -> p k a n", k=KC, a=2)
                wdma(wv[:, :, 0, :], w_fg[l].rr("(k p) n -> p k n", p=128)[:, :, jf * 128:(jf + 1) * 128])
                wdma(wv[:, :, 1, :], w_fu[l].rr("(k p) n -> p k n", p=128)[:, :, jf * 128:(jf + 1) * 128])
                for (t0, n) in tbs:
                    ps = PS[cnt['ps'] % 2]; cnt['ps'] += 1
                    proj_fm(ps[:, 0:n], wv[:, :, 0, :], t0, n)
                    p2 = PS[2 + cnt['ps'] % 2]
                    proj_fm(p2[:, 0:n], wv[:, :, 1, :], t0, n)
                    sg_ = sgt[cnt['ps'] % 2]
                    P.act(sg_[:, 0:n], ps[:, 0:n], AF.Silu)
                    P.tt_('dve', V(uT.full()[:, jl, t0:t0 + n], uT.root, uT), sg_[:, 0:n], p2[:, 0:n], ALU.mult)
            for h in range(2):
                wb = next_w()
                wdv = wb.v()[:, 0:nj * 512].rr("p (j n) -> p j n", j=nj)
                wdma(wdv, w_fd[l, j0 * 128:(j0 + nj) * 128, :].rr("(j p) n -> p j n", p=128)[:, :, h * 512:(h + 1) * 512])
                for t in range(t_lo, NT):
                    xv = V(X.full()[:, t, :], X.k(t).buf, X)
                    ps = PS[4 + cnt['ps'] % 2]; cnt['ps'] += 1
                    for jl in range(nj):
                        P.mm(ps.v(), V(uT.full()[:, jl, t * 128:(t + 1) * 128], uT.root, uT), wdv[:, jl, :],
                             start=(jl == 0), stop=(jl == nj - 1))
                    gb = (cg2b if t < 2 else g2b)[:, h * 512:(h + 1) * 512]
                    resid_add(ps.v(), gb, xv[:, h * 512:(h + 1) * 512], k); k += 1
        if not last:
            for t in range(t_lo, NT):
                P.dma(xs_d[t * 128:(t + 1) * 128, :], V(X.full()[:, t, :], X.k(t).buf, X), chan=xch[t])
        else:
            P.dma(g1b.v(), fng[0, :].pbc(128))
            for t in range(2, NT):
                xv = V(X.full()[:, t, :], X.k(t).buf, X)
                P.act(sqjR.v(), xv, AF.Square, accum=ss_t[:, t:t + 1])
                P.act(rs_t[:, t:t + 1], ss_t[:, t:t + 1], AF.Sqrt, bias=epsc.v(), scale=1.0 / D)
                P.recip(rs_t[:, t:t + 1], rs_t[:, t:t + 1])
                P.stt('dve', xv, xv, rs_t[:, t:t + 1], g1b.v(), ALU.mult, ALU.mult)
                P.dma(out_d[s, (t - 2) * 128:(t - 1) * 128, :], xv, chan=xch[t])

    P.marks = []

    def mark(lbl):
        P.marks.append((lbl, len(P.eng_ops['pe'])))
    for s in range(nseq):
        for l in range(nlayer):
            need_ctx = l < DEPTH - 1
            mark(f's{s}l{l}N1')
            for tn_ in [xscN[0], xscN[1], sqjN]:
                P.alias(tn_, [mT, uT] + R_TENS + P0_TENS)
            P.alias(X, R_TENS + P0_TENS)
            for t in range(NT):
                xv = V(X.full()[:, t, :], X.k(t).buf, X)
                P.dma(xv, xsrc(s, l, t), chan=xch[t])
                norm_stats(xv, t, sqjN)
            norm_rstd(0, NT)
            for t in range(NT):
                norm_apply(V(X.full()[:, t, :], X.k(t).buf, X), t, modT[(l, 2 if t < 2 else s)], 0, xscN[t % 2])
            if 'h' in dbg and s == 0:
                P.dump(f"d_hT{l}", hT.v(), [128, KC, T], BF16)
            if 'stopN1' in dbg:
                return P
            P.alias(aT, [X]); P.alias(bT, [X]); P.alias(cT, [X])
            mark(f's{s}l{l}A')
            phaseA(s, l, need_ctx)
            if 'a' in dbg and s == 0:
                P.dump(f"d_aT{l}", aT.v(), [128, 4, T], BF16)
                P.dump(f"d_qT{l}", qT.v(), [128, 4, T], BF16)
                P.dump(f"d_kT{l}", kTg[0].v(), [128, T], BF16)
            if 'stopA' in dbg:
                return P
            mark(f's{s}l{l}B')
            phaseB(s, l, need_ctx)
            if 'b' in dbg and s == 0:
                P.dump(f"d_bT{l}", bT.v(), [128, 4, T], BF16)
            if 'stopB' in dbg:
                return P
            mark(f's{s}l{l}C')
            phaseC(s, l, need_ctx)
            if 'c' in dbg and s == 0:
                P.dump(f"d_cT{l}", cT.v(), [128, 4, T], BF16)
            if 'stopC' in dbg:
                return P
            mark(f's{s}l{l}D')
            phaseD(s, l, need_ctx)
            if 'stopD' in dbg:
                return P
            mark(f's{s}l{l}E')
            phaseEFG(s, l, need_ctx, last=(l == DEPTH - 1))
            mark(f's{s}l{l}end')
            if 'x' in dbg and s == 0 and l == 0:
                P.dump(f"d_xs{l}", xs_d.v(), [T, D], F32)
    if 'stop0' in dbg:
        return P
    if 'mods' in dbg:
        P.dump("d_mods", modscr.v(), [DEPTH, 3, 6 * D], F32)
        P.dump("d_modT", modT[(0, 0)].v(), [128, 32], F32)
        P.dump("d_oml", oml.v(), [128, 16], F32)
    return P


_CACHE = {}


def make_in_maps(inputs):
    constf, constb, rope = host_consts()
    f = lambda a: np.ascontiguousarray(np.asarray(a, dtype=np.float32))
    shared = {
        "c_ctx": f(inputs["c_ctx"]).reshape(1, D),
        "w_ada": f(inputs["w_ada"]), "b_ada": f(inputs["b_ada"]),
        "norm1_g": f(inputs["norm1_g"]), "norm2_g": f(inputs["norm2_g"]),
        "w_in": f(inputs["w_in"]), "attn_sink": f(inputs["attn_sink"]),
        "hg_lb_logits": f(inputs["hg_lb_logits"]).reshape(DEPTH * 2 * 4, 128),
        "hg_norm_g": f(inputs["hg_norm_g"]).reshape(DEPTH * 4, 128),
        "mlp_v_norm_g": f(inputs["mlp_v_norm_g"]), "mlp_v_norm_b": f(inputs["mlp_v_norm_b"]),
        "mlp_ws": f(inputs["mlp_ws"]), "mlp_bs": f(inputs["mlp_bs"]),
        "w_branch": f(inputs["w_branch"]), "w_out": f(inputs["w_out"]),
        "w_ffn_gate": f(inputs["w_ffn_gate"]), "w_ffn_up": f(inputs["w_ffn_up"]),
        "w_ffn_down": f(inputs["w_ffn_down"]),
        "final_norm_g": f(inputs["final_norm_g"]).reshape(1, D),
        "constf": constf, "constb": constb, "rope": rope,
    }
    x = f(inputs["x"]); c = f(inputs["c"]); ctx = f(inputs["ctx"])
    maps = []
    for i in range(8):
        m = dict(shared)
        m["x"] = x[2 * i:2 * i + 2]
        m["c"] = c[2 * i:2 * i + 2]
        m["ctx"] = ctx[2 * i:2 * i + 2]
        maps.append(m)
    return maps


def kernel(**inputs):
    if "nc" not in _CACHE:
        _CACHE["nc"] = build().finalize()
    nc = _CACHE["nc"]
    maps = make_in_maps(inputs)
    res = run_bass_kernel_spmd(nc, maps, core_ids=list(range(8)))
    out = np.concatenate([r["out"] for r in res.results], axis=0)
    return out.astype(np.float32)
```

```python
import numpy as np
import ml_dtypes
import concourse.bass as bass
import concourse.mybir as mybir
from concourse.bass_utils import run_bass_kernel_spmd

F32 = mybir.dt.float32
BF16 = mybir.dt.bfloat16
AF = mybir.ActivationFunctionType
ALU = mybir.AluOpType

D = 1024
KC = 8
SEQ = 2048
CTX = 256
T = SEQ + CTX
NT = T // 128
DEPTH = 2
D_IN = 7424
D_FF = 2816
EPS = 1e-6
ENGS = ('pe', 'act', 'dve', 'pool')
EIDX = {e: i for i, e in enumerate(ENGS)}


class Buf:
    __slots__ = ('name', 'writers', 'wgrp', 'readers', 'parent', 'subs')

    def __init__(self, name, parent=None):
        self.name = name
        self.writers = []
        self.wgrp = None
        self.readers = {}
        self.parent = parent
        self.subs = {}

    def conflict(self):
        if self.parent is None:
            return [self] + list(self.subs.values())
        return [self, self.parent]

    def upd(self):
        if self.parent is None:
            return [self] + list(self.subs.values())
        return [self]


class Chan:
    __slots__ = ('sem', 'count', 'wait_all', 'name')

    def __init__(self, name, wait_all=False):
        self.name = name
        self.sem = None
        self.count = 0
        self.wait_all = wait_all


class Op:
    __slots__ = ('eng', 'fn', 'deps', 'pos', 'is_dma', 'chan', 'chan_need', 'waits',
                 'needs_inc', 'semval', 'clk', 'done', 'name')


class V:
    __slots__ = ('ap', 'buf', 'tt')

    def __init__(self, ap, buf, tt):
        self.ap = ap
        self.buf = buf
        self.tt = tt

    def __getitem__(self, idx):
        return V(self.ap[idx], self.buf, self.tt)

    def rr(self, s, **kw):
        return V(self.ap.rearrange(s, **kw), self.buf, self.tt)

    def bc(self, shape):
        return V(self.ap.to_broadcast(shape), self.buf, self.tt)

    def bitcast(self, dt):
        return V(self.ap.bitcast(dt), self.buf, self.tt)

    def unsq(self, ax):
        return V(self.ap.unsqueeze(ax), self.buf, self.tt)

    def pbc(self, n):
        return V(self.ap.partition_broadcast(n), self.buf, self.tt)


class TT:
    def __init__(self, handle, name, dram=False):
        self.h = handle
        self.name = name
        self.root = Buf(name)
        self.chan = None
        self.dram = dram

    def full(self):
        return self.h.ap() if self.dram else self.h[:]

    def __getitem__(self, idx):
        return V(self.full(), self.root, self)[idx]

    def k(self, key):
        b = self.root.subs.get(key)
        if b is None:
            b = Buf(f"{self.name}.{key}", self.root)
            self.root.subs[key] = b
        return V(self.full(), b, self)

    def v(self):
        return V(self.full(), self.root, self)


class Prog:
    def __init__(self):
        self.nc = bass.Bass("TRN2", target_bir_lowering=False)
        self.ops = []
        self.eng_ops = {e: [] for e in ENGS + ('sp',)}
        self.npos = {e: 0 for e in ENGS}
        self.chans = []
        self.misc = self.new_chan('misc', wait_all=True)
        self.sb_off = 0
        self.outs = []
        self.tok_prev = None
        self.tok_cur = None

    def new_chan(self, name, wait_all=False):
        c = Chan(name, wait_all)
        self.chans.append(c)
        return c

    def dram(self, name, shape, dt, kind="Internal"):
        return TT(self.nc.dram_tensor(name, list(shape), dt, kind=kind), name, dram=True)

    def sbuf(self, name, shape, dt, off=None):
        if off is None:
            h = self.nc.alloc_sbuf_tensor(name, list(shape), dt)
        else:
            h = self.nc.alloc_sbuf_tensor_at(name, list(shape), dt, offset=off)
        return TT(h, name)

    def psum(self, name, shape, dt=F32):
        return TT(self.nc.alloc_psum_tensor(name, list(shape), dt), name)

    def alias(self, new, olds):
        for o in olds:
            for b in o.root.conflict():
                for w in b.writers:
                    new.root.readers[('w', id(w))] = w
                for k, r in b.readers.items():
                    new.root.readers[(k, id(r))] = r

    def step_begin(self):
        self.tok_prev = self.tok_cur
        self.tok_cur = [len(self.ops), len(self.eng_ops['pool'])]

    def _add(self, eng, fn, reads, writes, is_dma=False, chan=None, grp=None, name=None, hoist=False):
        op = Op()
        op.eng = eng
        op.fn = fn
        op.is_dma = is_dma
        op.chan = chan
        op.name = name
        op.needs_inc = False
        op.waits = None
        deps = {}
        gonly = None
        if isinstance(grp, tuple) and len(grp) == 3 and grp[0] == 'only':
            gonly = grp[1]
            grp = grp[2]

        def gfor(b):
            return grp if (gonly is None or b is gonly) else None
        for v in reads:
            for b in v.buf.conflict():
                for w in b.writers:
                    deps[id(w)] = w
        same_ok = False
        for v in writes:
            for b in v.buf.conflict():
                if not (gfor(b) is not None and b is v.buf and b.wgrp == gfor(b)):
                    for w in b.writers:
                        if same_ok and (not w.is_dma) and w.eng == eng:
                            continue
                        deps[id(w)] = w
                for r in b.readers.values():
                    if same_ok and (not r.is_dma) and r.eng == eng:
                        continue
                    deps[id(r)] = r
        op.deps = list(deps.values())
        op.chan_need = {}
        for d in op.deps:
            if d.is_dma:
                op.chan_need[id(d.chan)] = (d.chan, d.chan.count)
        for v in writes:
            for b in v.buf.upd():
                if gfor(b) is not None and b is v.buf and b.wgrp == gfor(b):
                    b.writers.append(op)
                else:
                    b.writers = [op]
                    b.wgrp = gfor(b) if b is v.buf else None
                b.readers = {}
        rkey = chan if is_dma else eng
        for v in reads:
            for b in v.buf.upd():
                b.readers[id(rkey) if is_dma else rkey] = op
        if is_dma:
            chan.count += 1
            op.pos = chan.count
        else:
            self.npos[eng] += 1
            op.pos = self.npos[eng]
        if hoist and self.tok_prev is not None:
            gi, pi = self.tok_prev
            self.ops.insert(gi, op)
            self.eng_ops[eng].insert(pi, op)
            self.tok_prev[0] += 1; self.tok_prev[1] += 1
            self.tok_cur[0] += 1; self.tok_cur[1] += 1
        else:
            self.ops.append(op)
            self.eng_ops[eng].append(op)
        return op

    def dma(self, out, in_, q='sp', chan=None, grp=None, hoist=False):
        if chan is None:
            tt = out.tt if not out.tt.dram else in_.tt
            if tt.chan is None:
                tt.chan = self.new_chan(tt.name)
            chan = tt.chan
        oa, ia = out.ap, in_.ap
        return self._add(q, lambda e: e.dma_start(out=oa, in_=ia), [in_], [out], True, chan, grp, hoist=hoist)

    def mm(self, out, lhsT, rhs, start=True, stop=True, grp=None, tp=None):
        oa, la, ra = out.ap, lhsT.ap, rhs.ap
        kw = {}
        if tp is not None:
            kw['tile_position'] = tp
        return self._add('pe', lambda e: e.matmul(oa, la, ra, start=start, stop=stop, **kw),
                         [lhsT, rhs], [out], grp=grp)

    def tr(self, out, in_, ident, grp=None):
        oa, ia, da = out.ap, in_.ap, ident.ap
        return self._add('pe', lambda e: e.transpose(oa, ia, da), [in_, ident], [out], grp=grp)

    def act(self, out, in_, func, bias=None, scale=None, accum=None, grp=None):
        oa, ia = out.ap, in_.ap
        kw = {}
        reads = [in_]
        writes = [out]
        if bias is not None:
            if isinstance(bias, V):
                kw['bias'] = bias.ap
                reads.append(bias)
            else:
                kw['bias'] = float(bias)
        if scale is not None:
            if isinstance(scale, V):
                kw['scale'] = scale.ap
                reads.append(scale)
            else:
                kw['scale'] = float(scale)
        if accum is not None:
            kw['accum_out'] = accum.ap
            writes.append(accum)
        return self._add('act', lambda e: e.activation(oa, ia, func, **kw), reads, writes, grp=grp)

    def tt_(self, eng, out, a, b, op, grp=None):
        oa, aa, ba = out.ap, a.ap, b.ap
        return self._add(eng, lambda e: e.tensor_tensor(oa, aa, ba, op), [a, b], [out], grp=grp)

    def ts(self, eng, out, a, s1, s2=None, op0=ALU.mult, op1=None, grp=None):
        oa, aa = out.ap, a.ap
        reads = [a]
        s1a = s1.ap if isinstance(s1, V) else float(s1)
        if isinstance(s1, V):
            reads.append(s1)
        s2a = None
        if s2 is not None:
            s2a = s2.ap if isinstance(s2, V) else float(s2)
            if isinstance(s2, V):
                reads.append(s2)
        if op1 is None:
            return self._add(eng, lambda e: e.tensor_scalar(oa, aa, s1a, None, op0), reads, [out], grp=grp)
        return self._add(eng, lambda e: e.tensor_scalar(oa, aa, s1a, s2a, op0, op1), reads, [out], grp=grp)

    def stt(self, eng, out, a, s, b, op0, op1, grp=None):
        oa, aa, ba = out.ap, a.ap, b.ap
        reads = [a, b]
        sa = s.ap if isinstance(s, V) else float(s)
        if isinstance(s, V):
            reads.append(s)
        return self._add(eng, lambda e: e.scalar_tensor_tensor(oa, aa, sa, ba, op0, op1), reads, [out], grp=grp)

    def copy(self, eng, out, a, grp=None):
        oa, aa = out.ap, a.ap
        if eng == 'act':
            return self._add(eng, lambda e: e.copy(oa, aa), [a], [out], grp=grp)
        return self._add(eng, lambda e: e.tensor_copy(oa, aa), [a], [out], grp=grp)

    def recip(self, out, a, grp=None):
        oa, aa = out.ap, a.ap
        return self._add('dve', lambda e: e.reciprocal(oa, aa), [a], [out], grp=grp)

    def scan(self, out, d0, d1, init, op0, op1, grp=None):
        oa, a0, a1 = out.ap, d0.ap, d1.ap
        return self._add('dve', lambda e: e.tensor_tensor_scan(oa, a0, a1, float(init), op0, op1),
                         [d0, d1], [out], grp=grp)

    def memset(self, eng, out, val):
        oa = out.ap
        return self._add(eng, lambda e: e.memset(oa, val), [], [out])

    def bn_stats(self, out, a):
        oa, aa = out.ap, a.ap
        return self._add('dve', lambda e: e.bn_stats(oa, aa), [a], [out])

    def bn_aggr(self, out, a):
        oa, aa = out.ap, a.ap
        return self._add('dve', lambda e: e.bn_aggr(oa, aa), [a], [out])

    def dump(self, name, v, shape, dt):
        o = self.dram(name, shape, dt, kind="ExternalOutput")
        self.dma(o.v(), v)
        self.outs.append(name)

    def finalize(self):
        nc = self.nc
        last_clk = {e: (0, 0, 0, 0) for e in ENGS + ('sp',)}
        known = {e: {} for e in ENGS + ('sp',)}
        for op in self.ops:
            E = op.eng
            clk = list(last_clk[E])
            waits = []
            for cid, (c, need) in op.chan_need.items():
                if c.wait_all:
                    need = -1
                    if known[E].get(cid) == -1:
                        continue
                elif known[E].get(cid, 0) >= need:
                    continue
                known[E][cid] = need
                waits.append(('c', c, need))
            cdeps = [d for d in op.deps if not d.is_dma]
            cdeps.sort(key=lambda d: -d.pos)
            for d in cdeps:
                di = EIDX[d.eng]
                if clk[di] >= d.pos:
                    continue
                if d.eng == E and E == 'pe':
                    continue
                waits.append(('e', d))
                d.needs_inc = True
                dc = d.done
                clk = [max(a, b) for a, b in zip(clk, dc)]
            op.waits = waits
            op.clk = tuple(clk)
            last_clk[E] = op.clk
            if not op.is_dma:
                dn = list(clk)
                dn[EIDX[E]] = max(dn[EIDX[E]], op.pos)
                op.done = tuple(dn)
        sems = {e: nc.alloc_semaphore(f"s_{e}") for e in ENGS}
        for c in self.chans:
            if c.count > 0:
                c.sem = nc.alloc_semaphore(f"c_{c.name}")
        for e in ENGS:
            n = 0
            for op in self.eng_ops[e]:
                if not op.is_dma and op.needs_inc:
                    n += 1
                    op.semval = n
            self.__dict__.setdefault('semmax', {})[e] = n

        def emit(eng, ename):
            for op in self.eng_ops[ename]:
                for w in op.waits:
                    if w[0] == 'c':
                        c, need = w[1], w[2]
                        eng.wait_ge(c.sem, 16 * (c.count if need == -1 else need))
                    else:
                        d = w[1]
                        eng.wait_ge(sems[d.eng], d.semval)
                ins = op.fn(eng)
                if op.is_dma:
                    ins.then_inc(op.chan.sem, 16)
                elif op.needs_inc:
                    ins.then_inc(sems[ename], 1)
            if ename == 'sp':
                for c in self.chans:
                    if c.count > 0:
                        eng.wait_ge(c.sem, 16 * c.count)

        with nc.Block() as block:
            @block.tensor
            def _(eng):
                emit(eng, 'pe')

            @block.scalar
            def _(eng):
                emit(eng, 'act')

            @block.vector
            def _(eng):
                emit(eng, 'dve')

            @block.gpsimd
            def _(eng):
                emit(eng, 'pool')

            @block.sync
            def _(eng):
                emit(eng, 'sp')
        return nc


def host_consts():
    p = np.arange(128)
    ident = np.eye(128, dtype=np.float32)
    d = p % 64
    perm = np.where((d % 32) < 16, p + 16, p - 16)
    permT = np.zeros((128, 128), np.float32)
    permT[perm, p] = 1.0
    ones = np.ones((128, 128), np.float32)
    startm = np.zeros((128, 512), np.float32)
    startm[:, ::32] = 1.0
    cmask = (p[:, None] // 32 == np.arange(4)[None, :]).astype(np.float32)
    constf = np.concatenate([ident, permT, ones, startm, cmask], axis=1)
    kk = p[:, None]
    qq = p[None, :]
    maskPrev = (qq <= kk).astype(np.float32)
    maskNext = (kk <= qq).astype(np.float32)
    same = (kk // 32) == (qq // 32)
    maskF = (same & (kk <= qq)).astype(np.float32)
    maskB = (same & (kk >= qq)).astype(np.float32)
    constb = np.concatenate([ident, maskPrev, maskNext, maskF, maskB], axis=1).astype(ml_dtypes.bfloat16)
    n = np.arange(SEQ)
    rows = (n // 64).astype(np.float32)
    cols = (n % 64).astype(np.float32)
    inv = (10000.0 ** (-np.arange(16, dtype=np.float32) / 16)).astype(np.float32)
    C = np.zeros((128, SEQ), np.float32)
    S = np.zeros((128, SEQ), np.float32)
    for pp in range(128):
        dd = pp % 64
        pos = rows if dd < 32 else cols
        ang = (pos * inv[dd % 16]).astype(np.float32)
        C[pp] = np.cos(ang)
        S[pp] = np.sin(ang) * (-1.0 if (dd % 32) < 16 else 1.0)
    rope = np.concatenate([C, S], axis=1).astype(np.float32)
    return constf, constb, rope


ARENA = 212000
O_CF, O_CB, O_SM, O_HT, O_W0, O_W1, O_U = 0, 3616, 4896, 8992, 45856, 58144, 70432
O_WK = O_U + 55296
O_MT = O_U + 73728
O_TR = O_U + 110592


def build(dbg=None, nseq=2, nlayer=DEPTH):
    dbg = dbg or set()
    P = Prog()
    nc = P.nc
    arena = nc.alloc_sbuf_tensor("arena", [128, ARENA], mybir.dt.uint8)

    class AT(TT):
        def __init__(self, name, off, shape, dt):
            self.name = name
            self.root = Buf(name)
            self.chan = None
            self.dram = False
            esz = 4 if dt == F32 else 2
            n = int(np.prod(shape[1:]))
            assert off + n * esz <= ARENA, (name, off, n * esz)
            ap = arena[0:shape[0], off:off + n * esz].bitcast(dt)
            if len(shape) == 3:
                ap = ap.rearrange("p (a b) -> p a b", b=shape[2])
            elif len(shape) == 4:
                ap = ap.rearrange("p (a b c) -> p a b c", b=shape[2], c=shape[3])
            self._ap = ap
            self.off = off
            self.nbytes = n * esz

        def full(self):
            return self._ap

    def DI(name, shape, dt=F32):
        return P.dram(name, shape, dt, kind="ExternalInput")

    x_in = DI("x", [2, SEQ, D]); c_in = DI("c", [2, D]); ctx_in = DI("ctx", [2, CTX, D]); cctx_in = DI("c_ctx", [1, D])
    w_ada = DI("w_ada", [DEPTH, D, 6 * D]); b_ada = DI("b_ada", [DEPTH, 6 * D])
    n1g = DI("norm1_g", [DEPTH, D]); n2g = DI("norm2_g", [DEPTH, D])
    w_in = DI("w_in", [DEPTH, D, D_IN]); sink_in = DI("attn_sink", [DEPTH, 8])
    lbl = DI("hg_lb_logits", [DEPTH * 2 * 4, 128]); hgg = DI("hg_norm_g", [DEPTH * 4, 128])
    vg_in = DI("mlp_v_norm_g", [DEPTH, 512]); vb_in = DI("mlp_v_norm_b", [DEPTH, 512])
    ws_in = DI("mlp_ws", [DEPTH, 4, 128, 128]); bs_in = DI("mlp_bs", [DEPTH, 4, 128])
    w_br = DI("w_branch", [DEPTH, 3, 512, D]); w_out = DI("w_out", [DEPTH, D, D])
    w_fg = DI("w_ffn_gate", [DEPTH, D, D_FF]); w_fu = DI("w_ffn_up", [DEPTH, D, D_FF]); w_fd = DI("w_ffn_down", [DEPTH, D_FF, D])
    fng = DI("final_norm_g", [1, D])
    constf_d = DI("constf", [128, 900]); constb_d = DI("constb", [128, 640], BF16); rope_d = DI("rope", [128, 4096])
    out_d = P.dram("out", [2, SEQ, D], F32, kind="ExternalOutput")
    modscr = P.dram("modscr", [DEPTH, 3, 6 * D], F32)
    xs_d = P.dram("xs", [T, D], F32)

    constf = AT("constf", O_CF, [128, 900], F32)
    constb = AT("constb", O_CB, [128, 640], BF16)
    ident_f = constf[:, 0:128]; permT = constf[:, 128:256]; ones_f = constf[:, 256:384]; startm = constf[:, 384:896]; cmask = constf[:, 896:900]
    ident_b = constb[:, 0:128]; maskPrev = constb[:, 128:256]; maskNext = constb[:, 256:384]
    maskF = constb[:, 384:512]; maskB = constb[:, 512:640]
    P.dma(constf.v(), constf_d.v())
    P.dma(constb.v(), constb_d.v())
    sm_off = [O_SM]

    def SM(name, shape, dt=F32):
        esz = 4 if dt == F32 else 2
        n = int(np.prod(shape[1:])) * esz
        n = (n + 31) // 32 * 32
        t = AT(name, sm_off[0], shape, dt)
        sm_off[0] += n
        assert sm_off[0] <= O_HT, name
        return t

    hT = AT("hT", O_HT, [128, KC, T], BF16)
    W = [AT("W0", O_W0, [128, 6144], BF16), AT("W1", O_W1, [128, 6144], BF16)]
    wrot = [0]
    PS = [P.psum(f"ps{i}", [128, 512], F32) for i in range(8)]

    def wload(src_ap_fn):
        w = W[wrot[0] % 2]
        wrot[0] += 1
        return w

    def wdma(dst, src):
        P.dma(dst, src, q='pool', hoist=True)

    def next_w():
        P.step_begin()
        w = W[wrot[0] % 2]
        wrot[0] += 1
        return w

    ACTIVE_TB_FULL = [(0, 512), (512, 512), (1024, 512), (1536, 512), (2048, 256)]
    ACTIVE_TB_LAT = [(256, 512), (768, 512), (1280, 512), (1792, 512)]

    def hkeys(t0, n):
        return [hT.k(t) for t in range(t0 // 128, (t0 + n) // 128)]

    def proj_fm(psv, wv, t0, n, hsrc=None):
        hs = hsrc or hT
        for kc in range(KC):
            rhs = V(hs.full()[:, kc, t0:t0 + n], hs.root, hs)
            P.mm(psv, wv[:, kc, :], rhs, start=(kc == 0), stop=(kc == KC - 1))

    small = {}
    ld = AT("ld0", O_U, [128, 128], F32)
    nrow = 0
    stage_rows = {}

    def stage(name, src_v, n):
        nonlocal nrow
        P.dma(V(ld.full()[nrow:nrow + n, :], ld.root, ld), src_v)
        stage_rows[name] = (nrow, n)
        nrow += n
    stage('n1g', n1g.v().rr("l (j p) -> (l j) p", p=128), 16)
    stage('n2g', n2g.v().rr("l (j p) -> (l j) p", p=128), 16)
    stage('lbl', lbl.v(), 16)
    stage('hgg', hgg.v(), 8)
    vecT = SM("vecT", [128, 64])
    psA = PS[0]
    P.tr(psA[:, 0:nrow], ld[0:nrow, :], ident_f[0:nrow, 0:nrow])
    P.copy('dve', vecT[:, 0:nrow], psA[:, 0:nrow])

    if 'r1' in dbg:
        return P

    def vcol(name, i):
        r0, n = stage_rows[name]
        return vecT[:, r0 + i:r0 + i + 1]

    def vcols(name, i0, n):
        r0, _ = stage_rows[name]
        return vecT[:, r0 + i0:r0 + i0 + n]
    oml = SM("oml", [128, 16]); noml = SM("noml", [128, 16]); onec = SM("onec", [128, 1])
    P.memset('dve', onec.v(), 1.0)
    P.memset('dve', oml[:, 0:8], 1.0)
    dl = SM("dl", [128, 8])
    P.tt_('dve', dl.v(), vcols('lbl', 0, 8), vcols('lbl', 8, 8), ALU.subtract)
    P.act(oml[:, 8:16], dl.v(), AF.Sigmoid)
    P.ts('dve', noml.v(), oml.v(), -1.0)
    esink = SM("esink", [128, 16])
    P.dma(esink.v(), sink_in.v().rr("l h -> (l h)").pbc(128))
    P.act(esink.v(), esink.v(), AF.Exp)

    if 'r2' in dbg:
        return P
    crow = AT("crow", O_U + 65536, [3, D], F32)
    P.dma(crow[0:2, :], c_in.v())
    P.dma(crow[2:3, :], cctx_in.v())
    P.act(crow.v(), crow.v(), AF.Silu)
    scT = SM("scT", [128, KC, 4], BF16)
    psB = PS[1]
    for kc in range(KC):
        P.tr(psB[:, kc * 4:kc * 4 + 3], crow[0:3, kc * 128:(kc + 1) * 128], ident_f[0:3, 0:3])
    P.copy('dve', scT[:, :, 0:3], psB[:, 0:32].rr("p (k r) -> p k r", r=4)[:, :, 0:3])
    if 'r3' in dbg:
        return P
    modrow = AT("modrow", O_U + 69632, [3, 6 * D], F32)
    bias3 = AT("bias3", O_U + 69632 + 24576, [3, 6 * D], F32)
    modT = {}
    P0_TENS = [ld, crow, modrow, bias3]
    for l in range(nlayer):
        P.dma(bias3.v(), b_ada[l, :].pbc(3))
        for cb in range(12):
            wb = next_w()
            wv = wb.v().rr("p (k n) -> p k n", k=KC)[:, :, 0:512]
            wdma(wv, w_ada[l].rr("(k p) n -> p k n", p=128)[:, :, cb * 512:(cb + 1) * 512])
            ps = PS[2 + cb % 2]
            for kc in range(KC):
                P.mm(ps[0:3, :], scT[:, kc, 0:3], wv[:, kc, :], start=(kc == 0), stop=(kc == KC - 1))
            P.tt_('dve', modrow[:, cb * 512:(cb + 1) * 512], ps[0:3, :], bias3[:, cb * 512:(cb + 1) * 512], ALU.add)
        P.dma(modscr[l], modrow.v())
        if 'r4' in dbg:
            return P
        for r in range(3):
            mld = AT(f"mld{l}{r}", O_U + 2048 * (l * 3 + r), [48, 128], F32)
            P0_TENS.append(mld)
            P.dma(mld.v(), modscr[l, r, :].rr("(j p) -> j p", p=128))
            if 'r5' in dbg:
                return P
            ps = PS[4 + r % 2]
            P.tr(ps[:, 0:48], mld.v(), ident_f[0:48, 0:48])
            if 'r6' in dbg:
                return P
            mt = SM(f"modT{l}{r}", [128, 32])
            P.copy('dve', mt[:, 0:16], ps[:, 0:16])
            P.copy('dve', mt[:, 16:32], ps[:, 24:40])
            P.stt('dve', mt[:, 8:16], mt[:, 8:16], 1.0, vcols('n1g', l * 8, 8), ALU.add, ALU.mult)
            P.stt('dve', mt[:, 24:32], mt[:, 24:32], 1.0, vcols('n2g', l * 8, 8), ALU.add, ALU.mult)
            modT[(l, r)] = mt
            if 'r7' in dbg:
                return P
        if 'r8' in dbg:
            return P

    ss_t = SM("ss_t", [128, 24]); rs_t = SM("rs_t", [128, 24])
    den_t = SM("den_t", [128, 8]); rec_t = SM("rec_t", [128, 8])
    xin = [AT(f"xin{i}", O_WK + 4096 * i, [128, D], F32) for i in range(2)]
    xsc = [AT(f"xsc{i}", O_WK + 8192 + 4096 * i, [128, D], F32) for i in range(2)]
    sqj = AT("sqj", O_WK + 16384, [128, D], BF16)
    xscN = [AT(f"xscN{i}", O_MT + 4096 * i, [128, D], F32) for i in range(2)]
    sqjN = AT("sqjN", O_MT + 8192, [128, D], BF16)
    aT = AT("aT", O_U, [128, 4, T], BF16); bT = AT("bT", O_U + 18432, [128, 4, T], BF16); cT = AT("cT", O_U + 36864, [128, 4, T], BF16)
    mT = AT("mT", O_MT, [128, KC, T], BF16)
    X = AT("X", O_U, [128, NT, D], F32)
    ropeT = AT("ropeT", O_WK, [128, 4096], F32)
    qT = AT("qT", O_WK + 16384, [128, 4, T], BF16)
    kTg = [AT(f"kT{g}", O_WK + 34816 + 4608 * g, [128, T], BF16) for g in range(2)]
    kz = [[AT(f"kz{g}{a}", O_WK + 62080 + 4608 * (2 * g + a), [128, T], BF16) for a in range(2)] for g in range(2)]
    v_sb = AT("v_sb", O_WK + 44032, [128, NT, 2, 65], BF16)
    qf32 = [AT(f"qf32{i}", O_WK + 48768 + 2048 * i, [128, 512], F32) for i in range(2)]
    rt1 = AT("rt1", O_WK + 52864, [128, 512], F32); rt2 = AT("rt2", O_WK + 54912, [128, 512], F32)
    pTb = [AT(f"pT{i}", O_WK + 56960 + 1024 * i, [128, 4, 128], BF16) for i in range(3)]
    a_tok = [AT(f"a_tok{i}", O_WK + 60032 + 1024 * i, [128, 8, 64], BF16) for i in range(2)]
    cnt = {'x': 0, 'q': 0, 'p': 0, 'a': 0, 'ps': 0}

    def xsrc(s, l, t):
        if l == 0:
            if t < 2:
                return ctx_in[s, t * 128:(t + 1) * 128, :]
            return x_in[s, (t - 2) * 128:(t - 1) * 128, :]
        return xs_d[t * 128:(t + 1) * 128, :]

    def norm_stats(xv, t, sq_t):
        P.act(sq_t.v(), xv, AF.Square, accum=ss_t[:, t:t + 1], grp=('only', ss_t.root, 'ss'))

    def norm_rstd(t_lo, t_hi):
        P.act(rs_t[:, t_lo:t_hi], ss_t[:, t_lo:t_hi], AF.Sqrt, bias=epsc.v(), scale=1.0 / D)
        P.recip(rs_t[:, t_lo:t_hi], rs_t[:, t_lo:t_hi])

    def norm_apply(xv, t, mt, c0, xs):
        P.act(xs.v(), xv, AF.Copy, scale=rs_t[:, t:t + 1])
        for half in range(2):
            ps = PS[(t % 2) * 2 + half]
            for q in range(4):
                kc = half * 4 + q
                P.tr(ps[:, q * 128:(q + 1) * 128], xs[:, kc * 128:(kc + 1) * 128], ident_f)
            for q in range(4):
                kc = half * 4 + q
                dst = V(hT.full()[:, kc, t * 128:(t + 1) * 128], hT.k(t).buf, hT)
                if t % 2 == 0:
                    P.ts('dve', dst, ps[:, q * 128:(q + 1) * 128], mt[:, c0 + 8 + kc:c0 + 9 + kc], mt[:, c0 + kc:c0 + kc + 1],
                         ALU.mult, ALU.add, grp=('h', t))
                else:
                    P.act(dst, ps[:, q * 128:(q + 1) * 128], AF.Identity, bias=mt[:, c0 + kc:c0 + kc + 1],
                          scale=mt[:, c0 + 8 + kc:c0 + 9 + kc], grp=('h', t))

    epsc = SM("epsc", [128, 1])
    P.memset('dve', epsc.v(), EPS)
    negc = SM("negc", [128, 1])
    P.memset('dve', negc.v(), -1.0)

    def hslice(kc, t0, n):
        return V(hT.full()[:, kc, t0:t0 + n], hT.root, hT)

    def phaseA(s, l, need_ctx):
        for tn_ in A_TENS:
            P.alias(tn_, [X, xscN[0], xscN[1], sqjN])
        P.dma(ropeT.v(), rope_d.v())
        wl = w_in[l].rr("(k p) n -> p k n", p=128)
        wb = next_w()
        wv = wb.v()[:, 0:KC * 512].rr("p (k n) -> p k n", k=KC)
        wdma(wv, wl[:, :, 0:512])
        tbs = ACTIVE_TB_FULL if need_ctx else ACTIVE_TB_LAT

        def rope_block(ps, dstv, t0, n):
            i = cnt['q']; cnt['q'] += 1
            if t0 < CTX:
                nc_ = min(n, CTX - t0)
                P.copy('act', dstv[:, 0:nc_], ps[:, 0:nc_])
                if nc_ == n:
                    return
                a0 = nc_
            else:
                a0 = 0
            m = n - a0
            l0 = t0 + a0 - CTX
            qf = qf32[i % 2]
            P.copy('act', qf[:, 0:m], ps[:, a0:n])
            pp = PS[4 + i % 2]
            P.mm(pp[:, 0:m], permT, qf[:, 0:m])
            P.tt_('dve', rt1[:, 0:m], qf[:, 0:m], ropeT[:, l0:l0 + m], ALU.mult)
            P.tt_('dve', rt2[:, 0:m], pp[:, 0:m], ropeT[:, 2048 + l0:2048 + l0 + m], ALU.mult)
            P.tt_('dve', dstv[:, a0:n], rt1[:, 0:m], rt2[:, 0:m], ALU.add)

        def run_jobs(jobs):
            def proj(job, idx):
                wsl, dstv, t0, n = job
                ps = PS[2 + idx % 2]
                proj_fm(ps[:, 0:n], wsl, t0, n)
                return ps
            pend = proj(jobs[0], 0)
            for i, job in enumerate(jobs):
                nxt = proj(jobs[i + 1], i + 1) if i + 1 < len(jobs) else None
                rope_block(pend, job[1], job[2], job[3])
                pend = nxt

        run_jobs([(wv[:, :, j * 128:(j + 1) * 128], V(qT.full()[:, j, t0:t0 + n], qT.root, qT), t0, n)
                  for j in range(4) for (t0, n) in tbs])
        wb2 = next_w()
        wkv = wb2.v()[:, 0:KC * 384].rr("p (k n) -> p k n", k=KC)
        wk = wkv[:, :, 0:256]
        for g in range(2):
            for dup in range(2):
                wdma(wk[:, :, (g * 2 + dup) * 64:(g * 2 + dup + 1) * 64], wl[:, :, 512 + g * 64:512 + (g + 1) * 64])
        wdma(wkv[:, :, 256:384], wl[:, :, 640:768])
        run_jobs([(wk[:, :, g * 128:(g + 1) * 128], kTg[g][:, t0:t0 + n], t0, n)
                  for g in range(2) for (t0, n) in ACTIVE_TB_FULL])
        for g in range(2):
            for a in range(2):
                P.memset('dve', kz[g][a][(1 - a) * 64:(2 - a) * 64, :], 0.0)
                P.copy('dve' if a == 0 else 'act', kz[g][a][a * 64:(a + 1) * 64, :], kTg[g][a * 64:(a + 1) * 64, :])
        P.memset('dve', v_sb.v(), 1.0)
        for t in range(NT):
            ps = PS[2 + cnt['ps'] % 2]; cnt['ps'] += 1
            for kc in range(KC):
                P.mm(ps[:, 0:128], hslice(kc, t * 128, 128), wkv[:, kc, 256:384], start=(kc == 0), stop=(kc == KC - 1))
            P.copy('dve', v_sb[:, t, :, 0:64], ps[:, 0:128].rr("p (g d) -> p g d", g=2))
        if 'a2' in dbg:
            return
        blocks = []
        if need_ctx:
            for qt in range(2):
                blocks.append((qt, [(0, None), (1, None)]))
        for j in range(16):
            ks = []
            if j > 0:
                ks.append((2 + j - 1, maskPrev))
            ks.append((2 + j, None))
            if j < 15:
                ks.append((2 + j + 1, maskNext))
            ks += [(0, None), (1, None)]
            blocks.append((2 + j, ks))
        items = []
        for bi, (qt, ks) in enumerate(blocks):
            for g in range(2):
                for ci, (kt, mask) in enumerate(ks):
                    items.append((bi, qt, g, ci, kt, mask, len(ks)))

        def score(item, idx):
            bi, qt, g, ci, kt, mask, nks = item
            pss = PS[idx % 2]
            for a in range(2):
                P.mm(pss[:, a * 256:(a + 1) * 256].rr("p (b q) -> p b q", b=2),
                     kz[g][a][:, kt * 128:(kt + 1) * 128],
                     V(qT.full()[:, 2 * g:2 * g + 2, qt * 128:(qt + 1) * 128], qT.root, qT))
            return pss

        def rest(item, pss):
            bi, qt, g, ci, kt, mask, nks = item
            at = a_tok[bi % 2]
            po = PS[6 + g]
            pov = po[:, 0:260].rr("p (s d) -> p s d", d=65)
            pt = pTb[cnt['p'] % 3]; cnt['p'] += 1
            P.act(pt.v().rr("p s q -> p (s q)"), pss.v(), AF.Exp, scale=0.125)
            if mask is not None:
                P.tt_('dve', pt.v(), pt.v(), mask.unsq(1).bc([128, 4, 128]), ALU.mult)
            for sl in range(4):
                P.mm(pov[:, sl, :], pt[:, sl, :], v_sb[:, kt, g, :], start=(ci == 0 and sl == 0), stop=(ci == nks - 1 and sl == 3))
            if ci != nks - 1:
                return
            es = esink[:, l * 8 + 4 * g:l * 8 + 4 * g + 4].rr("p (b a) -> p a b", a=2)
            dn = den_t[:, 4 * g:4 * g + 4]
            P.tt_('dve', dn.rr("p (a b) -> p a b", a=2), pov[:, :, 64].rr("p (a b) -> p a b", a=2), es, ALU.add)
            P.recip(rec_t[:, 4 * g:4 * g + 4], dn)
            dst = at[:, 4 * g:4 * g + 4, :].rr("p (b a) d -> p a b d", a=2)
            P.tt_('dve', dst, pov[:, :, 0:64].rr("p (a b) d -> p a b d", a=2),
                  rec_t[:, 4 * g:4 * g + 4].rr("p (a b) -> p a b", a=2).unsq(3).bc([128, 2, 2, 64]), ALU.mult)
            if g != 1:
                return
            pt2 = PS[4 + bi % 2]
            ptb = pt2.v().bitcast(BF16)
            for jq in range(4):
                P.tr(ptb[:, jq * 128:(jq + 1) * 128], at[:, 2 * jq:2 * jq + 2, :].rr("p h d -> p (h d)"), ident_b)
            P.copy('act', V(aT.full()[:, :, qt * 128:(qt + 1) * 128], aT.root, aT), ptb[:, 0:512].rr("p (j q) -> p j q", j=4))

        pending = score(items[0], 0)
        for i, item in enumerate(items):
            nxt = score(items[i + 1], i + 1) if i + 1 < len(items) else None
            rest(item, pending)
            pending = nxt

    OB = O_WK + 64
    A_TENS = [ropeT, qT, kTg[0], kTg[1], kz[0][0], kz[0][1], kz[1][0], kz[1][1], v_sb, qf32[0], qf32[1], rt1, rt2] + pTb + a_tok
    QF, KF, KH, QB, QH, KB = [AT(nm, OB + 4608 * i, [128, T], BF16) for i, nm in enumerate(["QF", "KF", "KH", "QB", "QH", "KB"])]
    v_tok = AT("v_tok", OB + 27648, [128, NT, 128], BF16)
    gs = AT("gs", OB + 32256, [128, T], F32)
    o_acc = AT("o_acc", OB + 41472, [128, T], F32)
    tmpB = [AT(f"tmpB{i}", OB + 50688 + 2048 * i, [128, 512], F32) for i in range(8)]
    tmpB.append(AT("tmpB_fbs", OB + 50688 + 2048 * 8, [128, 528], F32))
    tmpB[3] = AT("tmpB_ffx", OB + 50688 + 2048 * 8 + 2112, [128, 528], F32)
    OC = O_U + 36864
    tmpB2 = [AT(f"tmpC{i}", OC + 2048 * i, [128, 512], F32) for i in range(8)]
    tmpB2[3] = AT("tmpC_ffx", OC + 2048 * 3, [128, 513], F32)
    for i_ in range(4, 8):
        tmpB2[i_] = AT(f"tmpC{i_}", OC + 2052 + 2048 * (i_ - 1), [128, 512], F32)
    tmpB2.append(AT("tmpC_fbs", OC + 2052 + 2048 * 7, [128, 513], F32))
    att_sb = [AT(f"att_sb{i}", OB + 71296 + 256 * i, [128, 128], BF16) for i in range(4)]
    kt_tok = [AT(f"kt_tok{i}", OB + 82112 + 1024 * i, [128, 4, 128], BF16) for i in range(4)]
    S_f = [[AT(f"S_f{d}{r}", OB + 78464 + 512 * (3 * d + r), [128, 128], F32) for r in range(3)] for d in range(2)]
    S_b = [[AT(f"S_b{d}{r}", OB + 72320 + 256 * (12 * d + r), [128, 128], BF16) for r in range(12)] for d in range(2)]
    EF = AT("EF", OB + 81536, [128, 72], F32); EB = AT("EB", OB + 81824, [128, 72], F32)
    iEF = AT("iEF", OB + 56832, [128, 72], F32); iEB = AT("iEB", OB + 57120, [128, 72], F32)
    B_TENS = [QF, KF, KH, QB, QH, KB, v_tok, gs, o_acc] + tmpB + tmpB2 + att_sb + kt_tok + S_f[0] + S_f[1] + S_b[0] + S_b[1] + [EF, EB, iEF, iEB]
    QSC = 128.0 ** -0.5

    def phaseB(s, l, need_ctx):
        for tnew in B_TENS:
            P.alias(tnew, A_TENS + [cT])
        wl = w_in[l].rr("(k p) n -> p k n", p=128)
        for tset in (tmpB, tmpB2):
            P.memset('dve', tset[8][:, 0:1], 1.0)
            P.memset('dve', tset[3].v(), 1.0)
        for hh in range(4):
            mark(f's{s}l{l}B{hh}p')
            wb = next_w()
            wv5 = wb.v()[:, 0:KC * 5 * 128].rr("p (k a n) -> p k a n", k=KC, a=5)
            for a in range(5):
                wdma(wv5[:, :, a, :], wl[:, :, 768 + a * 512 + hh * 128:768 + a * 512 + (hh + 1) * 128])
            omf = oml[:, l * 8 + hh:l * 8 + hh + 1]; nomf = noml[:, l * 8 + hh:l * 8 + hh + 1]
            omb = oml[:, l * 8 + 4 + hh:l * 8 + 5 + hh]; nomb = noml[:, l * 8 + 4 + hh:l * 8 + 5 + hh]
            for tbi, (t0, n) in enumerate(ACTIVE_TB_FULL):
                tq, sg, kf, ff, rf, ri, rho, tg, fbs = (tmpB if tbi % 2 == 0 else tmpB2)
                rib = ri; sgb = sg; kb = ff
                c0 = t0 // 32; ncn = n // 32
                ps = PS[cnt['ps'] % 4]; cnt['ps'] += 1
                proj_fm(ps[:, 0:n], wv5[:, :, 0, :], t0, n)
                P.act(tq[:, 0:n], ps[:, 0:n], AF.Silu)
                ps = PS[cnt['ps'] % 4]; cnt['ps'] += 1
                proj_fm(ps[:, 0:n], wv5[:, :, 1, :], t0, n)
                P.act(sg[:, 0:n], ps[:, 0:n], AF.Sigmoid, scale=-1.0)
                P.act(kf[:, 0:n], sg[:, 0:n], AF.Copy, scale=omf)
                P.act(ff[:, 0:n], sg[:, 0:n], AF.Identity, bias=onec.v(), scale=nomf)
                P.scan(rf[:, 0:n], startm[:, 0:n], ff[:, 0:n], 0.0, ALU.max, ALU.mult)
                P.scan(ri[:, 0:n][:, ::-1], ff[:, 1:n + 1][:, ::-1], startm[:, 0:n], 0.0, ALU.mult, ALU.max)
                P.stt('dve', QF[:, t0:t0 + n], tq[:, 0:n], QSC, rf[:, 0:n], ALU.mult, ALU.mult)
                P.copy('dve', EF[:, c0:c0 + ncn], rf[:, 0:n].rr("p (c t) -> p c t", t=32)[:, :, 31])
                P.tt_('pool', KH[:, t0:t0 + n], kf[:, 0:n], ri[:, 0:n], ALU.mult)
                ps = PS[cnt['ps'] % 4]; cnt['ps'] += 1
                proj_fm(ps[:, 0:n], wv5[:, :, 2, :], t0, n)
                P.act(sgb[:, 0:n], ps[:, 0:n], AF.Sigmoid, scale=-1.0)
                P.act(kb[:, 0:n], sgb[:, 0:n], AF.Copy, scale=omb)
                P.act(fbs[:, 1:n + 1], sgb[:, 0:n], AF.Identity, bias=onec.v(), scale=nomb)
                P.scan(rho[:, 0:n], fbs[:, 0:n], startm[:, 0:n], 0.0, ALU.mult, ALU.max)
                P.tt_('dve', KB[:, t0:t0 + n], kb[:, 0:n], rho[:, 0:n], ALU.mult)
                P.scan(rib[:, 0:n][:, ::-1], startm[:, 0:n], fbs[:, 1:n + 1][:, ::-1], 0.0, ALU.max, ALU.mult)
                P.copy('dve', EB[:, c0:c0 + ncn], rib[:, 0:n].rr("p (c t) -> p c t", t=32)[:, :, 0])
                P.stt('dve', QH[:, t0:t0 + n], tq[:, 0:n], QSC, rib[:, 0:n], ALU.mult, ALU.mult)
                ps = PS[cnt['ps'] % 4]; cnt['ps'] += 1
                proj_fm(ps[:, 0:n], wv5[:, :, 4, :], t0, n)
                P.act(tg[:, 0:n], ps[:, 0:n], AF.Silu)
                P.ts('dve', gs[:, t0:t0 + n], tg[:, 0:n], vcol('hgg', l * 4 + hh))
            P.recip(iEF.v(), EF.v())
            P.recip(iEB.v(), EB.v())
            for (t0, n) in ACTIVE_TB_FULL:
                c0 = t0 // 32; ncn = n // 32
                P.tt_('pool', KF[:, t0:t0 + n].rr("p (c t) -> p c t", t=32), KH[:, t0:t0 + n].rr("p (c t) -> p c t", t=32),
                      iEF[:, c0:c0 + ncn].unsq(2).bc([128, ncn, 32]), ALU.mult)
                P.tt_('pool', QB[:, t0:t0 + n].rr("p (c t) -> p c t", t=32), QH[:, t0:t0 + n].rr("p (c t) -> p c t", t=32),
                      iEB[:, c0:c0 + ncn].unsq(2).bc([128, ncn, 32]), ALU.mult)
            mark(f's{s}l{l}B{hh}v')
            for t in range(NT):
                ps = PS[2 + cnt['ps'] % 2]; cnt['ps'] += 1
                for kc in range(KC):
                    P.mm(ps[:, 0:128], hslice(kc, t * 128, 128), wv5[:, kc, 3, :], start=(kc == 0), stop=(kc == KC - 1))
                P.copy('act', v_tok[:, t, :], ps[:, 0:128])
            mark(f's{s}l{l}B{hh}r')
            P.memset('dve', o_acc.v(), 0.0)
            for d in range(2):
                P.memset('dve', S_f[d][0].v(), 0.0)
                P.memset('dve', S_b[d][0].v(), 0.0)
            border = [1, 0] + list(range(17, 1, -1))
            step = [0, 0]

            def cfg(d, i):
                t = i if d == 0 else border[i]
                if d == 0:
                    return t, QF, KF, KH, QF, EF, maskF, [0, 1, 2, 3]
                return t, QB, KB, KB, QH, EB, maskB, [3, 2, 1, 0]

            def stage1(i):
                for d in range(2):
                    t, Qa, Ka, Kkv, Qint, E, mask, order = cfg(d, i)
                    sl = slice(t * 128, (t + 1) * 128)
                    pa = PS[2]
                    P.mm(pa[:, d * 128:(d + 1) * 128], Ka[:, sl], Qa[:, sl])
                    att = att_sb[2 * d + i % 2]
                    P.tt_('dve', att.v(), pa[:, d * 128:(d + 1) * 128], mask, ALU.mult)
                    ptk = PS[3].v().bitcast(BF16)
                    P.tr(ptk[:, d * 128:(d + 1) * 128], Kkv[:, sl], ident_b)
                    kt = kt_tok[2 * d + i % 2]
                    for c in range(4):
                        P.act(kt[:, c, :], ptk[:, d * 128:(d + 1) * 128], AF.Copy, scale=cmask[:, c:c + 1])
                for d in range(2):
                    t, Qa, Ka, Kkv, Qint, E, mask, order = cfg(d, i)
                    kt = kt_tok[2 * d + i % 2]
                    pk = PS[[0, 1, 6, 7][2 * d + i % 2]]
                    for c in range(4):
                        P.mm(pk[:, c * 128:(c + 1) * 128], kt[:, c, :], v_tok[:, t, :])
                for ci in range(4):
                    for d in range(2):
                        t, Qa, Ka, Kkv, Qint, E, mask, order = cfg(d, i)
                        c = order[ci]
                        pk = PS[[0, 1, 6, 7][2 * d + i % 2]]
                        gc = t * 4 + c
                        k = step[d]
                        P.stt('dve', S_f[d][(k + 1) % 3].v(), S_f[d][k % 3].v(), E[:, gc:gc + 1], pk[:, c * 128:(c + 1) * 128], ALU.mult, ALU.add)
                        P.copy('pool', S_b[d][(k + 1) % 12].v(), S_f[d][(k + 1) % 3].v())
                        step[d] += 1

            def stage2(i):
                for d in range(2):
                    t, Qa, Ka, Kkv, Qint, E, mask, order = cfg(d, i)
                    sl = slice(t * 128, (t + 1) * 128)
                    att = att_sb[2 * d + i % 2]
                    po = PS[4 + d]
                    P.mm(po[:, 0:128], v_tok[:, t, :], att.v(), start=True, stop=False)
                    k0 = i * 4
                    for ci, c in enumerate(order):
                        cs = slice(t * 128 + c * 32, t * 128 + c * 32 + 32)
                        P.mm(po[:, c * 32:(c + 1) * 32], S_b[d][(k0 + ci) % 12].v(), Qint[:, cs], start=False, stop=(ci == 3))
                    P.tt_('dve', o_acc[:, sl], o_acc[:, sl], po[:, 0:128], ALU.add)

            for i in range(NT + 1):
                if i < NT:
                    stage1(i)
                if i >= 1:
                    stage2(i - 1)
            mark(f's{s}l{l}B{hh}o')
            tq, sg, kf = tmpB[0], tmpB[1], tmpB[2]
            for (t0, n) in (ACTIVE_TB_FULL if need_ctx else ACTIVE_TB_LAT):
                P.act(tq[:, 0:n], o_acc[:, t0:t0 + n], AF.Square)
                ps = PS[cnt['ps'] % 2]; cnt['ps'] += 1
                P.mm(ps[:, 0:n], ones_f, tq[:, 0:n])
                P.act(sg[:, 0:n], ps[:, 0:n], AF.Sqrt, bias=epsc.v(), scale=1.0 / 128)
                P.recip(sg[:, 0:n], sg[:, 0:n])
                P.tt_('dve', kf[:, 0:n], o_acc[:, t0:t0 + n], sg[:, 0:n], ALU.mult)
                P.tt_('dve', V(bT.full()[:, hh, t0:t0 + n], bT.root, bT), kf[:, 0:n], gs[:, t0:t0 + n], ALU.mult)
            if 'b' in dbg and s == 0 and hh == 0:
                P.dump(f"d_oacc{l}", o_acc.v(), [128, T], F32)

    vn_tok = AT("vn_tok", O_WK, [128, NT, 512], BF16)
    cvf = [AT(f"cvf{i}", O_WK + 18432 + 2048 * i, [128, 512], F32) for i in range(2)]
    s_sb = [AT(f"s_sb{i}", O_WK + 22528 + 2048 * i, [128, 512], F32) for i in range(2)]
    wsT = AT("wsT", O_WK + 26624, [128, 4, 128], BF16); ws_raw = AT("ws_raw", O_WK + 27648, [128, 4, 128], F32)
    bs_b = AT("bs_b", O_WK + 29696, [128, 4, 128], F32)
    vg_b = AT("vg_b", O_WK + 31744, [128, 512], F32); vb_b = AT("vb_b", O_WK + 33792, [128, 512], F32)
    smC = SM("smC", [128, 8])
    C_TENS = [vn_tok] + cvf + s_sb + [wsT, ws_raw, bs_b, vg_b, vb_b]

    def phaseC(s, l, need_ctx):
        for tnew in C_TENS:
            P.alias(tnew, B_TENS)
        P.alias(cT, tmpB2)
        wl = w_in[l].rr("(k p) n -> p k n", p=128)
        P.dma(ws_raw.v(), ws_in[l].rr("g t s -> t g s"))
        P.dma(bs_b.v().rr("p g t -> p (g t)"), bs_in[l].rr("g t -> (g t)").pbc(128))
        P.dma(vg_b.v(), vg_in[l, :].pbc(128))
        P.dma(vb_b.v(), vb_in[l, :].pbc(128))
        ps = PS[2]
        for g in range(4):
            P.tr(ps[:, g * 128:(g + 1) * 128], ws_raw[:, g, :], ident_f)
        P.copy('dve', wsT.v().rr("p g t -> p (g t)"), ps.v())
        wb = next_w()
        wcv = wb.v()[:, 0:KC * 512].rr("p (k n) -> p k n", k=KC)
        wdma(wcv, wl[:, :, 3840:4352])
        t_lo = 0 if need_ctx else 2
        for t in range(t_lo, NT):
            ps = PS[cnt['ps'] % 2]; cnt['ps'] += 1
            for kc in range(KC):
                P.mm(ps.v(), hslice(kc, t * 128, 128), wcv[:, kc, :], start=(kc == 0), stop=(kc == KC - 1))
            cv = cvf[t % 2]; jk = s_sb[t % 2]
            P.act(cv.v(), ps.v(), AF.Copy, accum=smC[:, 0:1])
            P.ts('dve', smC[:, 1:2], smC[:, 0:1], -1.0 / 512)
            P.act(jk.v(), cv.v(), AF.Square, bias=smC[:, 1:2], accum=smC[:, 2:3])
            P.act(smC[:, 3:4], smC[:, 2:3], AF.Sqrt, bias=epsc.v(), scale=1.0 / 512)
            P.recip(smC[:, 3:4], smC[:, 3:4])
            P.ts('dve', cv.v(), cv.v(), smC[:, 1:2], smC[:, 3:4], ALU.add, ALU.mult)
            P.tt_('dve', cv.v(), cv.v(), vg_b.v(), ALU.mult)
            P.tt_('dve', vn_tok[:, t, :], cv.v(), vb_b.v(), ALU.add)
        wb = next_w()
        wcu = wb.v()[:, 0:KC * 512].rr("p (k n) -> p k n", k=KC)
        wdma(wcu, wl[:, :, 3328:3840])
        for g in range(4):
            for (t0, n) in (ACTIVE_TB_FULL if need_ctx else ACTIVE_TB_LAT):
                ps = PS[cnt['ps'] % 2]; cnt['ps'] += 1
                proj_fm(ps[:, 0:n], wcu[:, :, g * 128:(g + 1) * 128], t0, n)
                p2 = PS[2 + cnt['ps'] % 2]
                nt_ = n // 128
                for i in range(nt_):
                    P.mm(p2[:, i * 128:(i + 1) * 128], vn_tok[:, t0 // 128 + i, g * 128:(g + 1) * 128], wsT[:, g, :])
                sb = s_sb[cnt['ps'] % 2]
                P.tt_('dve', sb[:, 0:n].rr("p (i t) -> p i t", t=128), p2[:, 0:n].rr("p (i t) -> p i t", t=128),
                      bs_b[:, g, :].unsq(1).bc([128, nt_, 128]), ALU.add)
                P.tt_('dve', V(cT.full()[:, g, t0:t0 + n], cT.root, cT), ps[:, 0:n], sb[:, 0:n], ALU.mult)

    sgD2 = [[AT(f"sgD{k}{i}", O_WK + (0 if k == 0 else 55296) + 2048 * i, [128, 512], F32) for i in range(3)] for k in range(2)]
    tmD2 = [[AT(f"tmD{k}{i}", O_WK + (0 if k == 0 else 55296) + 6144 + 2048 * i, [128, 512], F32) for i in range(3)] for k in range(2)]
    D_TENS = sgD2[0] + sgD2[1] + tmD2[0] + tmD2[1]

    def phaseD(s, l, need_ctx):
        for tnew in D_TENS + [mT]:
            P.alias(tnew, C_TENS + B_TENS)
        wl = w_in[l].rr("(k p) n -> p k n", p=128)
        wbl = w_br[l].rr("i (k p) n -> p k i n", p=128)
        brs = [aT, bT, cT]
        for nb in range(8):
            wb = next_w()
            wg3 = wb.v()[:, 0:KC * 384].rr("p (k i n) -> p k i n", k=KC, i=3)
            wbr = wb.v()[:, KC * 384:KC * 384 + 4 * 384].rr("p (k i n) -> p k i n", k=4, i=3)
            for i in range(3):
                wdma(wg3[:, :, i, :], wl[:, :, 4352 + i * 1024 + nb * 128:4352 + i * 1024 + (nb + 1) * 128])
                wdma(wbr[:, :, i, :], wbl[:, :, i, nb * 128:(nb + 1) * 128])
            for (t0, n) in (ACTIVE_TB_FULL if need_ctx else ACTIVE_TB_LAT):
                sgD = sgD2[cnt['a'] % 2]; tmD = tmD2[cnt['a'] % 2]; cnt['a'] += 1
                for i in range(3):
                    ps = PS[cnt['ps'] % 2]; cnt['ps'] += 1
                    proj_fm(ps[:, 0:n], wg3[:, :, i, :], t0, n)
                    P.act(sgD[i][:, 0:n], ps[:, 0:n], AF.Sigmoid)
                    p2 = PS[2 + cnt['ps'] % 2]
                    for kc in range(4):
                        P.mm(p2[:, 0:n], wbr[:, kc, i, :], V(brs[i].full()[:, kc, t0:t0 + n], brs[i].root, brs[i]),
                             start=(kc == 0), stop=(kc == 3))
                    P.tt_('dve', tmD[i][:, 0:n], sgD[i][:, 0:n], p2[:, 0:n], ALU.mult)
                P.tt_('pool', tmD[0][:, 0:n], tmD[0][:, 0:n], tmD[1][:, 0:n], ALU.add)
                P.tt_('pool', V(mT.full()[:, nb, t0:t0 + n], mT.root, mT), tmD[0][:, 0:n], tmD[2][:, 0:n], ALU.add)

    g1b = AT("g1b", O_TR, [128, D], F32); cg1b = AT("cg1b", O_TR + 4096, [128, D], F32)
    g2b = AT("g2b", O_TR + 8192, [128, D], F32); cg2b = AT("cg2b", O_TR + 12288, [128, D], F32)
    tmpE = [AT(f"tmpE{i}", O_TR + 16384 + 2048 * i, [128, 512], F32) for i in range(2)]
    xsR = AT("xsR", O_TR + 20480, [128, D], F32); sqjR = AT("sqjR", O_TR + 24576, [128, D], BF16)
    sgt = [AT(f"sgt{i}", O_TR + 26624 + 2048 * i, [128, 512], F32) for i in range(2)]
    xsR2 = AT("xsR2", O_TR + 26624, [128, D], F32)
    uT = AT("uT", O_MT, [128, KC, T], BF16)
    R_TENS = [g1b, cg1b, g2b, cg2b, xsR, sqjR] + tmpE + sgt
    xch = [P.new_chan(f"X{t}") for t in range(NT)]
    M_ALL = A_TENS + B_TENS + C_TENS + D_TENS + [aT, bT, cT, xscN[0], xscN[1], sqjN]

    def resid_add(psv, gb, xv, k):
        tm = tmpE[k % 2]
        P.tt_('dve', tm.v(), psv, gb, ALU.mult)
        P.tt_('pool', xv, xv, tm.v(), ALU.add)

    def phaseEFG(s, l, need_ctx, last):
        for tnew in R_TENS + [X]:
            P.alias(tnew, M_ALL)
        t_lo = 0 if need_ctx else 2
        P.dma(g1b.v(), modscr[l, s, 2048:3072].pbc(128))
        P.dma(g2b.v(), modscr[l, s, 5120:6144].pbc(128))
        if need_ctx:
            P.dma(cg1b.v(), modscr[l, 2, 2048:3072].pbc(128))
            P.dma(cg2b.v(), modscr[l, 2, 5120:6144].pbc(128))
        k = 0
        for h in range(2):
            wb = next_w()
            wov = wb.v()[:, 0:KC * 512].rr("p (k n) -> p k n", k=KC)
            wdma(wov, w_out[l].rr("(k p) n -> p k n", p=128)[:, :, h * 512:(h + 1) * 512])
            for t in range(t_lo, NT):
                xv = V(X.full()[:, t, :], X.k(t).buf, X)
                if h == 0:
                    P.dma(xv, xsrc(s, l, t), chan=xch[t])
                ps = PS[cnt['ps'] % 2]; cnt['ps'] += 1
                for kc in range(KC):
                    P.mm(ps.v(), V(mT.full()[:, kc, t * 128:(t + 1) * 128], mT.root, mT), wov[:, kc, :],
                         start=(kc == 0), stop=(kc == KC - 1))
                gb = (cg1b if t < 2 else g1b)[:, h * 512:(h + 1) * 512]
                resid_add(ps.v(), gb, xv[:, h * 512:(h + 1) * 512], k); k += 1
        if 'xm' in dbg and s == 0:
            P.dump(f"d_xm{l}", X.v(), [128, NT, D], F32)
        mark(f's{s}l{l}F')
        P.alias(xsR2, sgt)
        for t in range(t_lo, NT):
            norm_stats(V(X.full()[:, t, :], X.k(t).buf, X), t, sqjR)
        norm_rstd(t_lo, NT)
        for t in range(t_lo, NT):
            norm_apply(V(X.full()[:, t, :], X.k(t).buf, X), t, modT[(l, 2 if t < 2 else s)], 16, [xsR, xsR2][t % 2])
        for sg_ in sgt:
            P.alias(sg_, [xsR2])
        mark(f's{s}l{l}G')
        P.alias(uT, [mT])
        tbs = ACTIVE_TB_FULL if need_ctx else ACTIVE_TB_LAT
        for (j0, nj) in [(0, 8), (8, 7), (15, 7)]:
            for jl in range(nj):
                jf = j0 + jl
                wb = next_w()
                wv = wb.v()[:, 0:KC * 256].rr("p (k a n) -> p k a n", k=KC, a=2)
                wdma(wv[:, :, 0, :], w_fg[l].rr("(k p) n -> p k n", p=128)[:, :, jf * 128:(jf + 1) * 128])
                wdma(wv[:, :, 1, :], w_fu[l].rr("(k p) n -> p k n", p=128)[:, :, jf * 128:(jf + 1) * 128])
                for (t0, n) in tbs:
                    ps = PS[cnt['ps'] % 2]; cnt['ps'] += 1
                    proj_fm(ps[:, 0:n], wv[:, :, 0, :], t0, n)
                    p2 = PS[2 + cnt['ps'] % 2]
                    proj_fm(p2[:, 0:n], wv[:, :, 1, :], t0, n)
                    sg_ = sgt[cnt['ps'] % 2]
                    P.act(sg_[:, 0:n], ps[:, 0:n], AF.Silu)
                    P.tt_('dve', V(uT.full()[:, jl, t0:t0 + n], uT.root, uT), sg_[:, 0:n], p2[:, 0:n], ALU.mult)
            for h in range(2):
                wb = next_w()
                wdv = wb.v()[:, 0:nj * 512].rr("p (j n) -> p j n", j=nj)
                wdma(wdv, w_fd[l, j0 * 128:(j0 + nj) * 128, :].rr("(j p) n -> p j n", p=128)[:, :, h * 512:(h + 1) * 512])
                for t in range(t_lo, NT):
                    xv = V(X.full()[:, t, :], X.k(t).buf, X)
                    ps = PS[4 + cnt['ps'] % 2]; cnt['ps'] += 1
                    for jl in range(nj):
                        P.mm(ps.v(), V(uT.full()[:, jl, t * 128:(t + 1) * 128], uT.root, uT), wdv[:, jl, :],
                             start=(jl == 0), stop=(jl == nj - 1))
                    gb = (cg2b if t < 2 else g2b)[:, h * 512:(h + 1) * 512]
                    resid_add(ps.v(), gb, xv[:, h * 512:(h + 1) * 512], k); k += 1
        if not last:
            for t in range(t_lo, NT):
                P.dma(xs_d[t * 128:(t + 1) * 128, :], V(X.full()[:, t, :], X.k(t).buf, X), chan=xch[t])
        else:
            P.dma(g1b.v(), fng[0, :].pbc(128))
            for t in range(2, NT):
                xv = V(X.full()[:, t, :], X.k(t).buf, X)
                P.act(sqjR.v(), xv, AF.Square, accum=ss_t[:, t:t + 1])
                P.act(rs_t[:, t:t + 1], ss_t[:, t:t + 1], AF.Sqrt, bias=epsc.v(), scale=1.0 / D)
                P.recip(rs_t[:, t:t + 1], rs_t[:, t:t + 1])
                P.stt('dve', xv, xv, rs_t[:, t:t + 1], g1b.v(), ALU.mult, ALU.mult)
                P.dma(out_d[s, (t - 2) * 128:(t - 1) * 128, :], xv, chan=xch[t])

    P.marks = []

    def mark(lbl):
        P.marks.append((lbl, len(P.eng_ops['pe'])))
    for s in range(nseq):
        for l in range(nlayer):
            need_ctx = l < DEPTH - 1
            mark(f's{s}l{l}N1')
            for tn_ in [xscN[0], xscN[1], sqjN]:
                P.alias(tn_, [mT, uT] + R_TENS + P0_TENS)
            P.alias(X, R_TENS + P0_TENS)
            for t in range(NT):
                xv = V(X.full()[:, t, :], X.k(t).buf, X)
                P.dma(xv, xsrc(s, l, t), chan=xch[t])
                norm_stats(xv, t, sqjN)
            norm_rstd(0, NT)
            for t in range(NT):
                norm_apply(V(X.full()[:, t, :], X.k(t).buf, X), t, modT[(l, 2 if t < 2 else s)], 0, xscN[t % 2])
            if 'h' in dbg and s == 0:
                P.dump(f"d_hT{l}", hT.v(), [128, KC, T], BF16)
            if 'stopN1' in dbg:
                return P
            P.alias(aT, [X]); P.alias(bT, [X]); P.alias(cT, [X])
            mark(f's{s}l{l}A')
            phaseA(s, l, need_ctx)
            if 'a' in dbg and s == 0:
                P.dump(f"d_aT{l}", aT.v(), [128, 4, T], BF16)
                P.dump(f"d_qT{l}", qT.v(), [128, 4, T], BF16)
                P.dump(f"d_kT{l}", kTg[0].v(), [128, T], BF16)
            if 'stopA' in dbg:
                return P
            mark(f's{s}l{l}B')
            phaseB(s, l, need_ctx)
            if 'b' in dbg and s == 0:
                P.dump(f"d_bT{l}", bT.v(), [128, 4, T], BF16)
            if 'stopB' in dbg:
                return P
            mark(f's{s}l{l}C')
            phaseC(s, l, need_ctx)
            if 'c' in dbg and s == 0:
                P.dump(f"d_cT{l}", cT.v(), [128, 4, T], BF16)
            if 'stopC' in dbg:
                return P
            mark(f's{s}l{l}D')
            phaseD(s, l, need_ctx)
            if 'stopD' in dbg:
                return P
            mark(f's{s}l{l}E')
            phaseEFG(s, l, need_ctx, last=(l == DEPTH - 1))
            mark(f's{s}l{l}end')
            if 'x' in dbg and s == 0 and l == 0:
                P.dump(f"d_xs{l}", xs_d.v(), [T, D], F32)
    if 'stop0' in dbg:
        return P
    if 'mods' in dbg:
        P.dump("d_mods", modscr.v(), [DEPTH, 3, 6 * D], F32)
        P.dump("d_modT", modT[(0, 0)].v(), [128, 32], F32)
        P.dump("d_oml", oml.v(), [128, 16], F32)
    return P


_CACHE = {}


def make_in_maps(inputs):
    constf, constb, rope = host_consts()
    f = lambda a: np.ascontiguousarray(np.asarray(a, dtype=np.float32))
    shared = {
        "c_ctx": f(inputs["c_ctx"]).reshape(1, D),
        "w_ada": f(inputs["w_ada"]), "b_ada": f(inputs["b_ada"]),
        "norm1_g": f(inputs["norm1_g"]), "norm2_g": f(inputs["norm2_g"]),
        "w_in": f(inputs["w_in"]), "attn_sink": f(inputs["attn_sink"]),
        "hg_lb_logits": f(inputs["hg_lb_logits"]).reshape(DEPTH * 2 * 4, 128),
        "hg_norm_g": f(inputs["hg_norm_g"]).reshape(DEPTH * 4, 128),
        "mlp_v_norm_g": f(inputs["mlp_v_norm_g"]), "mlp_v_norm_b": f(inputs["mlp_v_norm_b"]),
        "mlp_ws": f(inputs["mlp_ws"]), "mlp_bs": f(inputs["mlp_bs"]),
        "w_branch": f(inputs["w_branch"]), "w_out": f(inputs["w_out"]),
        "w_ffn_gate": f(inputs["w_ffn_gate"]), "w_ffn_up": f(inputs["w_ffn_up"]),
        "w_ffn_down": f(inputs["w_ffn_down"]),
        "final_norm_g": f(inputs["final_norm_g"]).reshape(1, D),
        "constf": constf, "constb": constb, "rope": rope,
    }
    x = f(inputs["x"]); c = f(inputs["c"]); ctx = f(inputs["ctx"])
    maps = []
    for i in range(8):
        m = dict(shared)
        m["x"] = x[2 * i:2 * i + 2]
        m["c"] = c[2 * i:2 * i + 2]
        m["ctx"] = ctx[2 * i:2 * i + 2]
        maps.append(m)
    return maps


def kernel(**inputs):
    if "nc" not in _CACHE:
        _CACHE["nc"] = build().finalize()
    nc = _CACHE["nc"]
    maps = make_in_maps(inputs)
    res = run_bass_kernel_spmd(nc, maps, core_ids=list(range(8)))
    out = np.concatenate([r["out"] for r in res.results], axis=0)
    return out.astype(np.float32)
```
